# Optimizing a Trainium2 kernel written in Bass

```python
import math
import jax, jax.numpy as jnp
from jax import lax
import numpy as np

D_MODEL = 2048
BATCH = 16
SEQ = 2048
DEPTH = 1

GRID_W = 64
CTX_LEN = 256
N_MOD = 9
D_FF = 5504
A_HEADS = 8
A_HEAD_DIM = 128
A_DIM = A_HEADS * A_HEAD_DIM
A_CHUNK = 128
ROWS_PER_CHUNK = A_CHUNK // GRID_W
SSD_HEADS = 16
SSD_HEAD_DIM = 64
SSD_DIM = SSD_HEADS * SSD_HEAD_DIM
SSD_GROUPS = 2
SSD_HPG = SSD_HEADS // SSD_GROUPS
SSD_STATE = 128
SSD_CHUNK = 128
CONV_W = 5
XBC_DIM = SSD_DIM + 2 * SSD_GROUPS * SSD_STATE
MIX_DIM = A_DIM + SSD_DIM
IN_COLS = 2 * A_DIM + SSD_DIM + XBC_DIM + 2 * SSD_HEADS
EPS = 1e-6

kernel_name = 'hybrid_gmlp_ssd_macaron_prefix_block'


def _rms_norm(x, g):
    xf = x.astype(jnp.float32)
    y = xf * lax.rsqrt(jnp.mean(xf * xf, axis=-1, keepdims=True) + EPS)
    return (y * g.astype(jnp.float32)).astype(x.dtype)


def _layer_norm(x, g, b):
    xf = x.astype(jnp.float32)
    mu = jnp.mean(xf, axis=-1, keepdims=True)
    xc = xf - mu
    y = xc * lax.rsqrt(jnp.mean(xc * xc, axis=-1, keepdims=True) + EPS)
    return (y * g.astype(jnp.float32) + b.astype(jnp.float32)).astype(x.dtype)


def _modulate(h, shift, scale):
    return h * (1.0 + scale) + shift


def _half_ffn(x, mod, k, g, w_gate, w_up, w_down):
    h = _modulate(_rms_norm(x, g), mod[:, :, 3 * k], mod[:, :, 3 * k + 1])
    f = (jax.nn.silu(h @ w_gate) * (h @ w_up)) @ w_down
    return x + 0.5 * mod[:, :, 3 * k + 2] * f


def _dwconv_centred(x, w, b):
    ch = x.shape[-1]
    y = lax.conv_general_dilated(x, w[:, None, :].astype(x.dtype), window_strides=(1,),
                                 padding=[(CONV_W // 2, CONV_W // 2)],
                                 dimension_numbers=('NWC', 'WIO', 'NWC'),
                                 feature_group_count=ch)
    return y + b


def _chunk_mlp(u, v, n_chunks, g, beta, w_s, b_s):
    bsz = u.shape[0]
    v = _layer_norm(jax.nn.gelu(v, approximate=False), g, beta)
    vc = v.reshape(bsz, n_chunks, A_CHUNK, A_HEADS, A_HEAD_DIM)
    s = jnp.einsum('hqk,bckhd->bcqhd', w_s, vc) + b_s.T[:, :, None]
    return jax.nn.gelu(u, approximate=False) * s.reshape(bsz, n_chunks * A_CHUNK, A_DIM)


def _ssd_direction(xs, dt, A, Bm, Cm, h0, with_output):
    bsz, l, _, _ = xs.shape
    nc = l // SSD_CHUNK
    xdt = (xs * dt[..., None]).reshape(bsz, nc, SSD_CHUNK, SSD_GROUPS, SSD_HPG, SSD_HEAD_DIM)
    a_cum = jnp.cumsum((dt * A).reshape(bsz, nc, SSD_CHUNK, SSD_GROUPS, SSD_HPG), axis=2)
    Bc = Bm.reshape(bsz, nc, SSD_CHUNK, SSD_GROUPS, SSD_STATE)
    to_end = jnp.exp(a_cum[:, :, -1:] - a_cum)
    chunk_states = jnp.einsum('bcqgn,bcqgjp->bcgjpn', Bc, xdt * to_end[..., None])
    chunk_decay = jnp.exp(a_cum[:, :, -1])

    def step(h, inp):
        s, d = inp
        return d[..., None, None] * h + s, h

    final, prev = lax.scan(step, h0, (jnp.moveaxis(chunk_states, 1, 0),
                                      jnp.moveaxis(chunk_decay, 1, 0)))
    if not with_output:
        return None, final
    Cc = Cm.reshape(bsz, nc, SSD_CHUNK, SSD_GROUPS, SSD_STATE)
    cb = jnp.einsum('bcqgn,bckgn->bcgqk', Cc, Bc)
    seg = a_cum[:, :, :, None] - a_cum[:, :, None]
    mask = jnp.tril(jnp.ones((SSD_CHUNK, SSD_CHUNK), dtype=bool))[:, :, None, None]
    decay = jnp.where(mask, jnp.exp(jnp.where(mask, seg, 0.0)), 0.0)
    y_diag = jnp.einsum('bcgqk,bcqkgj,bckgjp->bcqgjp', cb, decay, xdt)
    y_off = jnp.einsum('bcqgn,cbgjpn->bcqgjp', Cc, prev) * jnp.exp(a_cum)[..., None]
    return (y_diag + y_off).reshape(bsz, l, SSD_HEADS, SSD_HEAD_DIM), final


def _flip(t):
    return jnp.flip(t, axis=1)


def _mixer(p, n_chunks, h0_f, h0_b, with_output, conv_w, conv_b, dt_bias, a_log, d_skip,
           ssd_norm_g, gmlp_norm_g, gmlp_norm_b, gmlp_w_s, gmlp_b_s, w_out):
    bsz, l, _ = p.shape
    o1 = 2 * A_DIM
    o2 = o1 + SSD_DIM
    o3 = o2 + XBC_DIM
    xbc = jax.nn.silu(_dwconv_centred(p[..., o2:o3], conv_w, conv_b)).astype(jnp.float32)
    xs = xbc[..., :SSD_DIM].reshape(bsz, l, SSD_HEADS, SSD_HEAD_DIM)
    gn = SSD_GROUPS * SSD_STATE
    Bm = xbc[..., SSD_DIM:SSD_DIM + gn].reshape(bsz, l, SSD_GROUPS, SSD_STATE)
    Cm = xbc[..., SSD_DIM + gn:].reshape(bsz, l, SSD_GROUPS, SSD_STATE)
    dt = jax.nn.softplus(p[..., o3:].astype(jnp.float32).reshape(bsz, l, 2, SSD_HEADS)
                         + dt_bias.astype(jnp.float32))
    A = -jnp.exp(a_log.astype(jnp.float32))
    y_f, fin_f = _ssd_direction(xs, dt[:, :, 0], A[0], Bm, Cm, h0_f, with_output)
    y_b, fin_b = _ssd_direction(_flip(xs), _flip(dt[:, :, 1]), A[1], _flip(Bm), _flip(Cm),
                                h0_b, with_output)
    if not with_output:
        return None, fin_f, fin_b
    y = (y_f + _flip(y_b) + d_skip.astype(jnp.float32)[:, None] * xs).reshape(bsz, l, SSD_DIM)
    yz = (y * jax.nn.silu(p[..., o1:o2].astype(jnp.float32))).reshape(bsz, l, SSD_GROUPS, -1)
    yz = yz * lax.rsqrt(jnp.mean(yz * yz, axis=-1, keepdims=True) + EPS)
    ssd_out = (yz.reshape(bsz, l, SSD_DIM) * ssd_norm_g.astype(jnp.float32)).astype(p.dtype)
    a_out = _chunk_mlp(p[..., :A_DIM], p[..., A_DIM:o1], n_chunks, gmlp_norm_g, gmlp_norm_b,
                       gmlp_w_s, gmlp_b_s)
    out = jnp.concatenate([a_out, ssd_out], axis=-1) @ w_out
    return out, fin_f, fin_b


def setup_inputs(seed: int = 0) -> dict:
    key = jax.random.key(seed)
    ks = jax.random.split(key, 32)

    def nrm(k, shape, scale):
        return jax.random.normal(k, shape, jnp.float32) * scale

    def gain(k, shape):
        return 1.0 + 0.02 * jax.random.normal(k, shape, jnp.float32)

    dt0 = jnp.exp(jax.random.uniform(ks[16], (DEPTH, 2, SSD_HEADS), jnp.float32,
                                     minval=math.log(1e-3), maxval=math.log(1e-1)))
    return {
        'x': nrm(ks[0], (BATCH, SEQ, D_MODEL), 1.0),
        'c': nrm(ks[1], (BATCH, D_MODEL), 1.0),
        'ctx': nrm(ks[2], (BATCH, CTX_LEN, D_MODEL), 1.0),
        'c_ctx': nrm(ks[3], (D_MODEL,), 1.0),
        'w_mod': nrm(ks[4], (DEPTH, D_MODEL, N_MOD * D_MODEL), 0.5 * D_MODEL ** -0.5),
        'b_mod': nrm(ks[5], (DEPTH, N_MOD * D_MODEL), 0.02),
        'norm_ffn1': gain(ks[6], (DEPTH, D_MODEL)),
        'ffn1_w_gate': nrm(ks[7], (DEPTH, D_MODEL, D_FF), D_MODEL ** -0.5),
        'ffn1_w_up': nrm(ks[8], (DEPTH, D_MODEL, D_FF), D_MODEL ** -0.5),
        'ffn1_w_down': nrm(ks[9], (DEPTH, D_FF, D_MODEL), D_FF ** -0.5),
        'norm_mix': gain(ks[10], (DEPTH, D_MODEL)),
        'w_in': nrm(ks[11], (DEPTH, D_MODEL, IN_COLS), D_MODEL ** -0.5),
        'conv_w': nrm(ks[12], (DEPTH, CONV_W, XBC_DIM), CONV_W ** -0.5),
        'conv_b': nrm(ks[13], (DEPTH, XBC_DIM), 0.02),
        'dt_bias': dt0 + jnp.log(-jnp.expm1(-dt0)),
        'a_log': jnp.log(jax.random.uniform(ks[14], (DEPTH, 2, SSD_HEADS), jnp.float32,
                                            minval=1.0, maxval=16.0)),
        'd_skip': gain(ks[15], (DEPTH, SSD_HEADS)),
        'ssd_norm_g': gain(ks[17], (DEPTH, SSD_DIM)),
        'gmlp_norm_g': gain(ks[18], (DEPTH, A_DIM)),
        'gmlp_norm_b': nrm(ks[19], (DEPTH, A_DIM), 0.02),
        'gmlp_w_s': nrm(ks[20], (DEPTH, A_HEADS, A_CHUNK, A_CHUNK), A_CHUNK ** -0.5),
        'gmlp_b_s': gain(ks[21], (DEPTH, A_HEADS, A_CHUNK)),
        'w_out': nrm(ks[22], (DEPTH, MIX_DIM, D_MODEL), MIX_DIM ** -0.5),
        'norm_ffn2': gain(ks[23], (DEPTH, D_MODEL)),
        'ffn2_w_gate': nrm(ks[24], (DEPTH, D_MODEL, D_FF), D_MODEL ** -0.5),
        'ffn2_w_up': nrm(ks[25], (DEPTH, D_MODEL, D_FF), D_MODEL ** -0.5),
        'ffn2_w_down': nrm(ks[26], (DEPTH, D_FF, D_MODEL), D_FF ** -0.5),
        'norm_final': gain(ks[27], (D_MODEL,)),
    }


def reference(x, c, ctx, c_ctx, w_mod, b_mod, norm_ffn1, ffn1_w_gate, ffn1_w_up, ffn1_w_down,
              norm_mix, w_in, conv_w, conv_b, dt_bias, a_log, d_skip, ssd_norm_g,
              gmlp_norm_g, gmlp_norm_b, gmlp_w_s, gmlp_b_s, w_out, norm_ffn2, ffn2_w_gate,
              ffn2_w_up, ffn2_w_down, norm_final):
    bsz = x.shape[0]
    rows = x.shape[1] // GRID_W
    lat_chunks = rows // ROWS_PER_CHUNK
    ctx_chunks = ctx.shape[1] // A_CHUNK
    h0 = jnp.zeros((bsz, SSD_GROUPS, SSD_HPG, SSD_HEAD_DIM, SSD_STATE), jnp.float32)
    for i in range(DEPTH):
        last = i == DEPTH - 1
        mod_x = (jax.nn.silu(c) @ w_mod[i] + b_mod[i]).reshape(bsz, 1, N_MOD, D_MODEL)
        mod_c = (jax.nn.silu(c_ctx) @ w_mod[i] + b_mod[i]).reshape(1, 1, N_MOD, D_MODEL)
        x = _half_ffn(x, mod_x, 0, norm_ffn1[i], ffn1_w_gate[i], ffn1_w_up[i], ffn1_w_down[i])
        ctx = _half_ffn(ctx, mod_c, 0, norm_ffn1[i], ffn1_w_gate[i], ffn1_w_up[i], ffn1_w_down[i])
        px = _modulate(_rms_norm(x, norm_mix[i]), mod_x[:, :, 3], mod_x[:, :, 4]) @ w_in[i]
        pc = _modulate(_rms_norm(ctx, norm_mix[i]), mod_c[:, :, 3], mod_c[:, :, 4]) @ w_in[i]
        mp = (conv_w[i], conv_b[i], dt_bias[i], a_log[i], d_skip[i], ssd_norm_g[i],
              gmlp_norm_g[i], gmlp_norm_b[i], gmlp_w_s[i], gmlp_b_s[i], w_out[i])
        out_c, h_f, h_b = _mixer(pc, ctx_chunks, h0, h0, not last, *mp)
        out_x, _, _ = _mixer(px, lat_chunks, h_f, h_b, True, *mp)
        x = x + mod_x[:, :, 5] * out_x
        x = _half_ffn(x, mod_x, 2, norm_ffn2[i], ffn2_w_gate[i], ffn2_w_up[i], ffn2_w_down[i])
        if not last:
            ctx = ctx + mod_c[:, :, 5] * out_c
            ctx = _half_ffn(ctx, mod_c, 2, norm_ffn2[i], ffn2_w_gate[i], ffn2_w_up[i],
                            ffn2_w_down[i])
    return _rms_norm(x, norm_final)
```

```python
import numpy as np
from contextlib import ExitStack
import concourse.bass as bass
import concourse.mybir as mybir
from concourse.bass_utils import run_bass_kernel_spmd

F32 = mybir.dt.float32
BF16 = mybir.dt.bfloat16
AF = mybir.ActivationFunctionType
ALU = mybir.AluOpType

D = 2048
FC = 16
DFF = 5504
NFF = 43
SEQ = 2048
CTX = 256
NCH = SEQ // 128
T = 512
EPS = 1e-6
NEG = -30000.0
GC = 4
NSLOT = 16

O_NG = 0
O_BMOD = 64
O_CW = 208
O_CB = 268
O_DTB = 280
O_ALOG = 312
O_DSK = 344
O_SSDG = 360
O_GNG = 368
O_GNB = 376
O_CT = 384
O_C128 = 432
SPW = 432 + 640


class Buf:
    __slots__ = ("name", "w", "r", "rd")

    def __init__(self, name=""):
        self.name = name
        self.w = None
        self.r = {}
        self.rd = []


class Ins:
    __slots__ = ("eng", "fn", "deps", "signal", "val", "dsem")


class Prog:
    ENG = ("pe", "act", "dve", "pool", "sp")

    def __init__(self):
        self.streams = {e: [] for e in self.ENG}
        self.dma_count = {}
        self.last_dma = {}

    def _dep(self, ins, p, kind):
        if p is ins:
            return
        pd = p.dsem is not None
        idm = ins.dsem is not None
        if not pd and not idm and p.eng == ins.eng:
            if kind != "RAW" or ins.eng == "pe":
                return
        if p not in ins.deps:
            ins.deps.append(p)
        if not pd:
            p.signal = True

    def op(self, eng, fn, reads=(), writes=(), dsem=None):
        ins = Ins()
        ins.eng = eng
        ins.fn = fn
        ins.deps = []
        ins.signal = False
        ins.val = None
        ins.dsem = dsem
        if dsem is not None:
            self.dma_count[dsem] = self.dma_count.get(dsem, 0) + 16
            ins.val = self.dma_count[dsem]
            self.last_dma[dsem] = ins
        for b in reads:
            if b.w is not None:
                self._dep(ins, b.w, "RAW")
        for b in writes:
            if b.w is not None:
                self._dep(ins, b.w, "WAW")
            for r in b.r.values():
                self._dep(ins, r, "WAR")
            for r in b.rd:
                self._dep(ins, r, "WAR")
        for b in reads:
            if dsem is not None:
                b.rd.append(ins)
            else:
                b.r[eng] = ins
        for b in writes:
            b.w = ins
            b.r = {}
            b.rd = []
        self.streams[eng].append(ins)
        return ins

    def barrier(self):
        lasts = []
        for e in self.ENG:
            for ins in reversed(self.streams[e]):
                if ins.dsem is None and ins.fn is not None:
                    ins.signal = True
                    lasts.append(ins)
                    break
        lasts += list(self.last_dma.values())
        for e in self.ENG:
            ins = Ins()
            ins.eng = e
            ins.fn = None
            ins.deps = list(lasts)
            ins.signal = False
            ins.val = None
            ins.dsem = None
            self.streams[e].append(ins)

    def emit(self, nc):
        with ExitStack() as es:
            esem = {e: es.enter_context(nc.semaphore("s_" + e)) for e in self.ENG}
            dsem = {k: es.enter_context(nc.semaphore("d_%d" % i))
                    for i, k in enumerate(self.dma_count)}
            for e in self.ENG:
                cnt = 0
                for ins in self.streams[e]:
                    if ins.dsem is None and ins.signal:
                        cnt += 1
                        ins.val = cnt
            block = es.enter_context(nc.Block())

            def run(e, engobj):
                waited = {}
                for ins in self.streams[e]:
                    for p in ins.deps:
                        if p.dsem is not None:
                            key = ("d", p.dsem)
                            sem = dsem[p.dsem]
                        else:
                            key = ("e", p.eng)
                            sem = esem[p.eng]
                        if waited.get(key, 0) >= p.val:
                            continue
                        waited[key] = p.val
                        engobj.wait_ge(sem, p.val)
                    if ins.fn is None:
                        continue
                    h = ins.fn(engobj)
                    if ins.dsem is not None:
                        h.then_inc(dsem[ins.dsem], 16)
                    elif ins.signal:
                        h.then_inc(esem[e], 1)

            @block.tensor
            def _(t):
                run("pe", t)

            @block.scalar
            def _(s):
                run("act", s)

            @block.vector
            def _(v):
                run("dve", v)

            @block.gpsimd
            def _(g):
                run("pool", g)

            @block.sync
            def _(sp):
                run("sp", sp)


def build_program(debug=False, arena_words=51 * 1024, phases=4):
    nc = bass.Bass("TRN2", target_bir_lowering=False)
    P = Prog()

    def din(name, shape, dt=F32):
        return nc.dram_tensor(name, list(shape), dt, kind="ExternalInput").ap()

    skind = "ExternalOutput" if debug else "Internal"

    def dscr(name, shape, dt=F32):
        return nc.dram_tensor(name, list(shape), dt, kind=skind).ap()

    xT_d = din("xT", [2, FC, 128, SEQ])
    ctxT_d = din("ctxT", [FC, 128, 2 * CTX])
    sp_d = din("sp", [128, SPW])
    mx_d = din("mx", [128, 2048])
    wmod_d = din("wmod", [144, 128, 16, 128])
    wg_d = [din("wg1", [NFF, 128, 16, 128]), din("wg2", [NFF, 128, 16, 128])]
    wu_d = [din("wu1", [NFF, 128, 16, 128]), din("wu2", [NFF, 128, 16, 128])]
    wd_d = [din("wd1", [NFF, 128, D]), din("wd2", [NFF, 128, D])]
    winfm_d = din("winfm", [20, 128, 16, 128])
    wintm_d = din("wintm", [16, 128, 16, 128])
    windt_d = din("windt", [128, 16, 32])
    wout_d = din("wout", [16, 128, D])
    outT_d = nc.dram_tensor("outT", [2, FC, 128, SEQ], F32, kind="ExternalOutput").ap()

    x1s = dscr("x1s", [2, FC, 128, SEQ])
    uTs = dscr("uTs", [2, 8, 128, SEQ])
    vs = dscr("vs", [2, SEQ, 1024])
    zs = dscr("zs", [2, SEQ, 1024])
    dts = dscr("dts", [2, SEQ, 32])
    dtc = dscr("dtc", [2, CTX, 32])
    xpre = dscr("xpre", [2, 12, 128, SEQ + 4])
    xprec = dscr("xprec", [2, 12, 128, CTX + 4])
    xpost = dscr("xpost", [2, 12, 128, SEQ], BF16)
    mixs = dscr("mixs", [2, FC, 128, SEQ], BF16)

    with ExitStack() as es:
        arena = es.enter_context(nc.sbuf_tensor("arena", [128, arena_words], F32))
        top = [0]

        def alloc(shape, dt=F32, parts=128):
            n = int(np.prod(shape))
            w = n if dt == F32 else (n + 1) // 2
            w = (w + 7) // 8 * 8
            o = top[0]
            top[0] += w
            assert top[0] <= arena_words, ("SBUF arena overflow", top[0], arena_words)
            ap = arena[0:parts, o:o + w]
            if dt != F32:
                ap = ap.bitcast(dt)
            ap = ap[:, 0:n]
            if len(shape) == 2:
                ap = ap.rearrange("p (a b) -> p a b", b=shape[1])
            elif len(shape) == 3:
                ap = ap.rearrange("p (a b c) -> p a b c", b=shape[1], c=shape[2])
            return ap

        banks = [es.enter_context(nc.psum_tensor("bank%d" % i, [128, 512], F32)) for i in range(8)]
        bk = [Buf("bank%d" % i) for i in range(8)]

        def bank_bf(i):
            return banks[i][:, :].bitcast(BF16)

        SPt = alloc([SPW])
        bSP = Buf("SP")
        ident_bf = alloc([128], BF16)
        ones_bf = alloc([128], BF16)
        ones_f = alloc([128], F32)
        zero_f = alloc([128], F32)
        modT = alloc([9, 16, 3])
        der = alloc([9, 16, 3])
        Ab = alloc([32])
        bConst = Buf("consts")
        bMod = Buf("mod")
        bDer = Buf("der")
        persist_top = top[0]

        ng = SPt[:, O_NG:O_NG + 64].rearrange("p (a b) -> p a b", b=16)
        bmodT = SPt[:, O_BMOD:O_BMOD + 144]
        cw = SPt[:, O_CW:O_CW + 60].rearrange("p (a b) -> p a b", b=5)
        cb = SPt[:, O_CB:O_CB + 12]
        dtb = SPt[:, O_DTB:O_DTB + 32]
        alog = SPt[:, O_ALOG:O_ALOG + 32]
        dsk = SPt[:, O_DSK:O_DSK + 16]
        ssdg = SPt[:, O_SSDG:O_SSDG + 8]
        gng = SPt[:, O_GNG:O_GNG + 8]
        gnb = SPt[:, O_GNB:O_GNB + 8]
        cT = SPt[:, O_CT:O_CT + 48].rearrange("p (a b) -> p a b", b=3)
        c128 = SPt[:, O_C128:O_C128 + 640].rearrange("p (a b) -> p a b", b=128)
        ident_f, triL, triU, mgt, mlt = (c128[:, i, :] for i in range(5))

        class Ring:
            def __init__(self, n):
                self.n = n
                self.base = top[0]
                self.slots = [alloc([2048], BF16) for _ in range(n)]
                assert top[0] == self.base + n * 1024
                self.bufs = [Buf("ws%d" % i) for i in range(n)]
                self.i = 0

            def load(self, src, kc_view):
                s = self.i % self.n
                self.i += 1
                ap = self.slots[s]
                if kc_view:
                    ap = ap.rearrange("p (a b) -> p a b", b=128)
                P.op("pool", lambda e, ap=ap, src=src: e.dma_start(out=ap, in_=src),
                     writes=[self.bufs[s]], dsem=("w", s))
                return ap, self.bufs[s]

        def ring_group4(srcs):
            while ring.i % 4 != 0:
                ring.i += 1
            s0 = ring.i % ring.n
            bufs = []
            for src in srcs:
                _, b = ring.load(src, True)
                bufs.append(b)
            o = ring.base + s0 * 1024
            grp = arena[:, o:o + 4096].bitcast(BF16).rearrange("p (j k c) -> p j k c", j=4, k=16, c=128)
            return grp, bufs

        P.op("sp", lambda e: e.dma_start(out=SPt, in_=sp_d), writes=[bSP], dsem="ld0")
        P.op("dve", lambda e: e.tensor_copy(out=ident_bf, in_=ident_f), reads=[bSP], writes=[bConst])
        P.op("dve", lambda e: e.memset(ones_bf, 1.0 / 2048.0), writes=[bConst])
        P.op("dve", lambda e: e.memset(ones_f, 1.0), writes=[bConst])
        P.op("dve", lambda e: e.memset(zero_f, 0.0), writes=[bConst])
        P.op("act", lambda e: e.activation(out=Ab, in_=alog, func=AF.Exp), reads=[bSP], writes=[bConst])
        P.op("dve", lambda e: e.tensor_scalar(out=Ab, in0=Ab, scalar1=-1.0, scalar2=None, op0=ALU.mult),
             reads=[bConst], writes=[bConst])
        for b2 in range(2):
            for (dst, L) in ((xpre, SEQ), (xprec, CTX)):
                for o in (0, L + 2):
                    P.op("sp", lambda e, dst=dst, b2=b2, o=o: e.dma_start(
                        out=dst[b2][:, :, o:o + 2].rearrange("c p t -> p c t"),
                        in_=zero_f[:, 0:24].rearrange("p (c t) -> p c t", t=2)),
                        reads=[bConst], dsem="pad")

        m0 = top[0]
        ring = Ring(NSLOT)
        scT = alloc([16, 3], BF16)
        bsc = Buf("scT")
        P.op("act", lambda e: e.activation(out=scT, in_=cT, func=AF.Silu), reads=[bSP], writes=[bsc])
        mod_ps = banks[7]
        for ch in range(144):
            wt, wb = ring.load(wmod_d[ch], True)
            for kc in range(16):
                P.op("pe", lambda e, wt=wt, kc=kc, ch=ch: e.matmul(
                    mod_ps[:, ch * 3:(ch + 1) * 3], lhsT=wt[:, kc, :], rhs=scT[:, kc, :],
                    start=(kc == 0), stop=(kc == 15)), reads=[wb, bsc], writes=[bk[7]])
        modT_f = modT.rearrange("p a b c -> p (a b) c")
        P.op("dve", lambda e: e.tensor_tensor(
            out=modT_f, in0=mod_ps[:, 0:432].rearrange("p (a c) -> p a c", c=3),
            in1=bmodT.unsqueeze(2).to_broadcast([128, 144, 3]), op=ALU.add),
            reads=[bk[7], bSP], writes=[bMod])
        for m in range(9):
            if m in (1, 4, 7):
                k = m // 3
                P.op("dve", lambda e, m=m: e.tensor_scalar(out=der[:, m], in0=modT[:, m], scalar1=1.0,
                                                           scalar2=None, op0=ALU.add),
                     reads=[bMod], writes=[bDer])
                P.op("dve", lambda e, m=m, k=k: e.tensor_tensor(
                    out=der[:, m], in0=der[:, m],
                    in1=ng[:, k, :].unsqueeze(2).to_broadcast([128, 16, 3]), op=ALU.mult),
                    reads=[bDer, bSP], writes=[bDer])
            elif m in (2, 8):
                P.op("dve", lambda e, m=m: e.tensor_scalar(out=der[:, m], in0=modT[:, m], scalar1=0.5,
                                                           scalar2=None, op0=ALU.mult),
                     reads=[bMod], writes=[bDer])
            else:
                P.op("dve", lambda e, m=m: e.tensor_copy(out=der[:, m], in_=modT[:, m]),
                     reads=[bMod], writes=[bDer])

        def sc(m, fc, r):
            return der[:, m, fc, r:r + 1]

        xT2 = [alloc([16, T]) for _ in range(2)]
        bx2 = [[Buf("x%d_%d" % (j, i)) for i in range(16)] for j in range(2)]
        cur = {"x": xT2[0], "bx": bx2[0]}
        hT = alloc([16, T], BF16)
        bh = [Buf("h%d" % i) for i in range(16)]
        AT = alloc([2 * GC, T], BF16)
        bA = [Buf("A%d" % i) for i in range(2 * GC)]
        sq = alloc([4, T], BF16)
        bsq = [Buf("sq%d" % i) for i in range(4)]
        tt = alloc([2, T])
        btt = [Buf("tt0"), Buf("tt1")]
        sg = alloc([2, T])
        bsg = [Buf("sg0"), Buf("sg1")]
        rstd = alloc([T])
        brs = Buf("rstd")
        stg = alloc([4, 512])
        bst = [Buf("st%d" % i) for i in range(4)]
        wdt = alloc([16, 32], BF16)
        bwdt = Buf("wdt")
        mixT = alloc([16, T], BF16)
        bmx = [Buf("mx%d" % i) for i in range(16)]
        ffn_top = top[0]
        stg_i = [0]
        P.op("pool", lambda e: e.dma_start(out=wdt, in_=windt_d), writes=[bwdt], dsem="wdt")

        def norm_prep(k, r, mg, ms):
            xTt, bx = cur["x"], cur["bx"]
            for fc in range(16):
                s = fc % 4
                if fc % 2 == 0:
                    P.op("act", lambda e, fc=fc, s=s: e.activation(out=sq[:, s, :], in_=xTt[:, fc, :], func=AF.Square),
                         reads=[bx[fc]], writes=[bsq[s]])
                else:
                    P.op("dve", lambda e, fc=fc, s=s: e.tensor_tensor(out=sq[:, s, :], in0=xTt[:, fc, :],
                                                                      in1=xTt[:, fc, :], op=ALU.mult),
                         reads=[bx[fc]], writes=[bsq[s]])
                P.op("pe", lambda e, fc=fc, s=s: e.matmul(banks[6][:, 0:T], lhsT=ones_bf, rhs=sq[:, s, :],
                                                          start=(fc == 0), stop=(fc == 15)),
                     reads=[bsq[s], bConst], writes=[bk[6]])
            P.op("act", lambda e: e.activation(out=rstd, in_=banks[6][:, 0:T], func=AF.Sqrt, bias=EPS, scale=1.0),
                 reads=[bk[6]], writes=[brs])
            P.op("dve", lambda e: e.reciprocal(out=rstd, in_=rstd), reads=[brs], writes=[brs])
            if mg is None:
                return
            for fc in range(16):
                s = fc % 2
                P.op("dve", lambda e, fc=fc, s=s: e.tensor_tensor(out=tt[:, s, :], in0=xTt[:, fc, :], in1=rstd,
                                                                  op=ALU.mult),
                     reads=[bx[fc], brs], writes=[btt[s]])
                P.op("act", lambda e, fc=fc, s=s: e.activation(out=hT[:, fc, :], in_=tt[:, s, :], func=AF.Identity,
                                                               scale=sc(mg, fc, r), bias=sc(ms, fc, r)),
                     reads=[btt[s], bDer], writes=[bh[fc]])

        def passB(chunks, rhs_of, rbufs_of, mscale, r, wsrc):
            xTt, bx = cur["x"], cur["bx"]
            slots = [ring.load(wsrc(c), False) for c in chunks]
            for dc in range(16):
                fb = 4 + dc % 4
                for i, c in enumerate(chunks):
                    wt, wb = slots[i]
                    P.op("pe", lambda e, wt=wt, dc=dc, i=i, c=c, fb=fb, n=len(chunks): e.matmul(
                        banks[fb][:, 0:T], lhsT=wt[:, dc * 128:(dc + 1) * 128], rhs=rhs_of(c),
                        start=(i == 0), stop=(i == n - 1)),
                        reads=[wb] + rbufs_of(c), writes=[bk[fb]])
                P.op("dve", lambda e, dc=dc, fb=fb: e.scalar_tensor_tensor(
                    out=xTt[:, dc, :], in0=banks[fb][:, 0:T], scalar=sc(mscale, dc, r), in1=xTt[:, dc, :],
                    op0=ALU.mult, op1=ALU.add),
                    reads=[bk[fb], bx[dc], bDer], writes=[bx[dc]])

        def ffn(k, r):
            wi = 0 if k == 0 else 1
            norm_prep(k, r, 3 * k + 1, 3 * k)
            groups = [list(range(g, min(g + GC, NFF))) for g in range(0, NFF, GC)]

            def passA(gi):
                for i, c in enumerate(groups[gi]):
                    a = (gi % 2) * GC + i
                    pb = c % 2
                    wgt, wgb = ring.load(wg_d[wi][c], True)
                    wut, wub = ring.load(wu_d[wi][c], True)
                    for kc in range(16):
                        P.op("pe", lambda e, wgt=wgt, kc=kc, pb=pb: e.matmul(
                            banks[pb][:, 0:T], lhsT=wgt[:, kc, :], rhs=hT[:, kc, :],
                            start=(kc == 0), stop=(kc == 15)), reads=[wgb, bh[kc]], writes=[bk[pb]])
                    for kc in range(16):
                        P.op("pe", lambda e, wut=wut, kc=kc, pb=pb: e.matmul(
                            banks[2 + pb][:, 0:T], lhsT=wut[:, kc, :], rhs=hT[:, kc, :],
                            start=(kc == 0), stop=(kc == 15)), reads=[wub, bh[kc]], writes=[bk[2 + pb]])
                    P.op("act", lambda e, pb=pb: e.activation(out=sg[:, pb, :], in_=banks[pb][:, 0:T], func=AF.Silu),
                         reads=[bk[pb]], writes=[bsg[pb]])
                    P.op("dve", lambda e, pb=pb, a=a: e.tensor_tensor(out=AT[:, a, :], in0=sg[:, pb, :],
                                                                      in1=banks[2 + pb][:, 0:T], op=ALU.mult),
                         reads=[bsg[pb], bk[2 + pb]], writes=[bA[a]])

            def doB(gi):
                base = (gi % 2) * GC
                g0 = groups[gi][0]
                passB(groups[gi], lambda c: AT[:, base + (c - g0), :], lambda c: [bA[base + (c - g0)]],
                      3 * k + 2, r, lambda c: wd_d[wi][c])

            passA(0)
            for gi in range(1, len(groups)):
                passA(gi)
                doB(gi - 1)
            doB(len(groups) - 1)

        def stage():
            i = stg_i[0] % 4
            stg_i[0] += 1
            return stg[:, i, :], (bst[i], ("st", i))

        def inproj(r, is_ctx, bi, t0):
            norm_prep(1, r, 4, 3)
            chs = list(range(8, 20)) if is_ctx else list(range(20))
            for n, ch in enumerate(chs):
                wt, wb = ring.load(winfm_d[ch], True)
                fb = 4 + n % 2
                for kc in range(16):
                    P.op("pe", lambda e, wt=wt, kc=kc, fb=fb: e.matmul(
                        banks[fb][:, 0:T], lhsT=wt[:, kc, :], rhs=hT[:, kc, :],
                        start=(kc == 0), stop=(kc == 15)), reads=[wb, bh[kc]], writes=[bk[fb]])
                st, sb = stage()
                if n % 2 == 0:
                    P.op("act", lambda e, st=st, fb=fb: e.activation(out=st, in_=banks[fb][:, 0:T], func=AF.Copy),
                         reads=[bk[fb]], writes=[sb[0]])
                else:
                    P.op("dve", lambda e, st=st, fb=fb: e.tensor_copy(out=st, in_=banks[fb][:, 0:T]),
                         reads=[bk[fb]], writes=[sb[0]])
                if ch < 8:
                    P.op("sp", lambda e, st=st, ch=ch: e.dma_start(out=uTs[bi][ch][:, t0:t0 + T], in_=st),
                         reads=[sb[0]], dsem=sb[1])
                elif not is_ctx:
                    P.op("sp", lambda e, st=st, ch=ch: e.dma_start(out=xpre[bi][ch - 8][:, 2 + t0:2 + t0 + T], in_=st),
                         reads=[sb[0]], dsem=sb[1])
                else:
                    for s2 in range(2):
                        P.op("sp", lambda e, st=st, ch=ch, s2=s2: e.dma_start(
                            out=xprec[s2][ch - 8][:, 2:2 + CTX], in_=st[:, s2 * CTX:(s2 + 1) * CTX]),
                            reads=[sb[0]], dsem=sb[1])
            if not is_ctx:
                n = 0
                for g4 in range(4):
                    grp, gb = ring_group4([wintm_d[4 * g4 + j] for j in range(4)])
                    dst = vs if g4 < 2 else zs
                    c0 = (g4 % 2) * 512
                    for m in range(T // 128):
                        fb = 4 + n % 2
                        for kc in range(16):
                            P.op("pe", lambda e, grp=grp, kc=kc, fb=fb, m=m: e.matmul(
                                banks[fb][:, :], lhsT=hT[:, kc, m * 128:(m + 1) * 128],
                                rhs=grp[:, :, kc, :], start=(kc == 0), stop=(kc == 15)),
                                reads=gb + [bh[kc]], writes=[bk[fb]])
                        st, sb = stage()
                        if n % 2 == 0:
                            P.op("act", lambda e, st=st, fb=fb: e.activation(out=st, in_=banks[fb][:, :], func=AF.Copy),
                                 reads=[bk[fb]], writes=[sb[0]])
                        else:
                            P.op("dve", lambda e, st=st, fb=fb: e.tensor_copy(out=st, in_=banks[fb][:, :]),
                                 reads=[bk[fb]], writes=[sb[0]])
                        P.op("sp", lambda e, st=st, dst=dst, c0=c0, m=m: e.dma_start(
                            out=dst[bi][t0 + m * 128:t0 + (m + 1) * 128, c0:c0 + 512], in_=st),
                            reads=[sb[0]], dsem=sb[1])
                        n += 1
            nm = T // 128
            for m in range(nm):
                for kc in range(16):
                    P.op("pe", lambda e, kc=kc, m=m: e.matmul(
                        banks[7][:, m * 32:(m + 1) * 32], lhsT=hT[:, kc, m * 128:(m + 1) * 128],
                        rhs=wdt[:, kc, :], start=(kc == 0), stop=(kc == 15)),
                        reads=[bwdt, bh[kc]], writes=[bk[7]])
            st, sb = stage()
            P.op("dve", lambda e, st=st: e.tensor_copy(out=st[:, 0:nm * 32], in_=banks[7][:, 0:nm * 32]),
                 reads=[bk[7]], writes=[sb[0]])
            if not is_ctx:
                P.op("sp", lambda e, st=st: e.dma_start(
                    out=dts[bi][t0:t0 + T, :].rearrange("(m p) c -> p m c", p=128),
                    in_=st[:, 0:nm * 32].rearrange("p (m c) -> p m c", c=32)), reads=[sb[0]], dsem=sb[1])
            else:
                for s2 in range(2):
                    P.op("sp", lambda e, st=st, s2=s2: e.dma_start(
                        out=dtc[s2].rearrange("(m p) c -> p m c", p=128),
                        in_=st[:, s2 * 64:(s2 + 1) * 64].rearrange("p (m c) -> p m c", c=32)),
                        reads=[sb[0]], dsem=sb[1])

        tiles = [("ctx", 0, 0)] + [("lat", bi, t0) for bi in range(2) for t0 in range(0, SEQ, T)]
        if phases >= 1:
            def load1(i):
                kind, bi, t0 = tiles[i]
                if kind == "ctx":
                    src = ctxT_d.rearrange("c p t -> p c t")
                else:
                    src = xT_d[bi][:, :, t0:t0 + T].rearrange("c p t -> p c t")
                P.op("sp", lambda e, src=src, i=i: e.dma_start(out=xT2[i % 2], in_=src), writes=bx2[i % 2],
                     dsem=("xl", i % 2))
            load1(0)
            for i, (kind, bi, t0) in enumerate(tiles):
                r = 2 if kind == "ctx" else bi
                if i + 1 < len(tiles):
                    load1(i + 1)
                cur["x"], cur["bx"] = xT2[i % 2], bx2[i % 2]
                ffn(0, r)
                inproj(r, kind == "ctx", bi, t0)
                if kind == "lat":
                    P.op("sp", lambda e, bi=bi, t0=t0, i=i: e.dma_start(
                        out=x1s[bi][:, :, t0:t0 + T].rearrange("c p t -> p c t"), in_=xT2[i % 2]),
                        reads=bx2[i % 2], dsem=("xs", i % 2))
        P.barrier()

        if phases >= 2:
            top[0] = persist_top
            mixer(nc, P, alloc, banks, bk, bank_bf, dict(
                SPt=SPt, bSP=bSP, bConst=bConst, ident_bf=ident_bf, top=top,
                ones_f=ones_f, zero_f=zero_f, ident_f=ident_f, Ab=Ab, cw=cw, cb=cb, dtb=dtb, dsk=dsk, ssdg=ssdg, gng=gng,
                gnb=gnb, triL=triL, triU=triU, mgt=mgt, mlt=mlt, mx_d=mx_d, uTs=uTs, vs=vs, zs=zs, dts=dts,
                dtc=dtc, xpre=xpre, xprec=xprec, xpost=xpost, mixs=mixs))
            P.barrier()
            top[0] = ffn_top

        if phases >= 3:
            t3 = [(bi, t0) for bi in range(2) for t0 in range(0, SEQ, T)]

            def load3x(i):
                bi, t0 = t3[i]
                P.op("sp", lambda e, bi=bi, t0=t0, i=i: e.dma_start(
                    out=xT2[i % 2], in_=x1s[bi][:, :, t0:t0 + T].rearrange("c p t -> p c t")),
                    writes=bx2[i % 2], dsem=("xl", i % 2))

            def load3m(i):
                bi, t0 = t3[i]
                P.op("sp", lambda e, bi=bi, t0=t0: e.dma_start(
                    out=mixT, in_=mixs[bi][:, :, t0:t0 + T].rearrange("c p t -> p c t")),
                    writes=bmx, dsem="ml")
            load3x(0)
            load3m(0)
            for i, (bi, t0) in enumerate(t3):
                if i + 1 < len(t3):
                    load3x(i + 1)
                cur["x"], cur["bx"] = xT2[i % 2], bx2[i % 2]
                xTt, bx = cur["x"], cur["bx"]
                for g in range(4):
                    passB(list(range(4 * g, 4 * g + 4)), lambda c: mixT[:, c, :], lambda c: [bmx[c]],
                          5, bi, lambda c: wout_d[c])
                if i + 1 < len(t3):
                    load3m(i + 1)
                ffn(2, bi)
                norm_prep(3, bi, None, None)
                for fc in range(16):
                    s = fc % 2
                    P.op("dve", lambda e, fc=fc, s=s, xTt=xTt: e.tensor_tensor(out=tt[:, s, :], in0=xTt[:, fc, :],
                                                                               in1=rstd, op=ALU.mult),
                         reads=[bx[fc], brs], writes=[btt[s]])
                    P.op("act", lambda e, fc=fc, s=s, xTt=xTt: e.activation(out=xTt[:, fc, :], in_=tt[:, s, :],
                                                                            func=AF.Copy, scale=ng[:, 3, fc:fc + 1]),
                         reads=[btt[s], bSP], writes=[bx[fc]])
                P.op("sp", lambda e, bi=bi, t0=t0, xTt=xTt: e.dma_start(
                    out=outT_d[bi][:, :, t0:t0 + T].rearrange("c p t -> p c t"), in_=xTt),
                    reads=bx, dsem=("out", i % 2))
        P.barrier()
        P.emit(nc)
    return nc


def _interleave(*gens):
    gens = [g for g in gens if g is not None]
    while gens:
        for g in list(gens):
            try:
                next(g)
            except StopIteration:
                gens.remove(g)


def mixer(nc, P, alloc, banks, bk, bank_bf, c):
    SPt, bSP, bConst = c["SPt"], c["bSP"], c["bConst"]
    ident_bf, ones_f = c["ident_bf"], c["ones_f"]
    Ab, cw, cb, dtb, dsk, ssdg, gng, gnb = c["Ab"], c["cw"], c["cb"], c["dtb"], c["dsk"], c["ssdg"], c["gng"], c["gnb"]
    triL, triU, mgt, mlt = c["triL"], c["triU"], c["mgt"], c["mlt"]
    top = c["top"]

    wsT = alloc([8, 128], BF16); bws = Buf("wsT")
    Bc = alloc([8, 128]); bBc = Buf("Bc")
    prevF = alloc([16, 1024], BF16); bpF = [Buf("pF%d" % i) for i in range(16)]
    prevB = alloc([16, 1024], BF16); bpB = [Buf("pB%d" % i) for i in range(16)]
    dcyB = alloc([16, 16]); bdB = [Buf("dB%d" % i) for i in range(16)]
    dtraw = alloc([18, 32]); bdtraw = Buf("dtraw")
    dtall = alloc([18, 32]); bdtall = Buf("dtall")
    aall = alloc([18, 32]); baall = Buf("aall")
    xbc = [alloc([12, 128], BF16) for _ in range(2)]; bxbc = [Buf("xbc0"), Buf("xbc1")]
    acum = [alloc([64]) for _ in range(2)]; bacum = [Buf("ac0"), Buf("ac1")]
    E32 = [alloc([32]) for _ in range(2)]; bE32 = [Buf("E0"), Buf("E1")]
    xstok = [alloc([1024], BF16) for _ in range(2)]; bxs = [Buf("xs0"), Buf("xs1")]
    xdtf = [alloc([1024], BF16) for _ in range(2)]; bxdf = [Buf("xf0"), Buf("xf1")]
    xdtb = [alloc([1024], BF16) for _ in range(2)]; bxdb = [Buf("xb0"), Buf("xb1")]
    xsd = [alloc([1024], BF16) for _ in range(2)]; bxsd = [Buf("xsd0"), Buf("xsd1")]
    cbTm = [alloc([2, 2, 128], BF16) for _ in range(2)]; bcbT = [Buf("cb0"), Buf("cb1")]
    MT = [alloc([32, 128], BF16) for _ in range(2)]
    bMT = [[Buf("MT%d_%d" % (s, i)) for i in range(8)] for s in range(2)]
    m_alias = top[0]
    MX = alloc([2048]); bMX = Buf("MX")
    bsb = MX[:, 0:1024].rearrange("p (h q) -> p h q", q=128)
    wsT_f = MX[:, 1024:2048]
    top[0] = m_alias
    H = [alloc([1024]) for _ in range(3)]; bH = [Buf("H%d" % i) for i in range(3)]
    htmp = alloc([1024]); bht = Buf("htmp")
    xp = [alloc([12, 132]) for _ in range(2)]; bxp = [Buf("xp0"), Buf("xp1")]
    acc = alloc([12, 128]); bacc = Buf("acc")
    ctmp = alloc([12, 128]); bctmp = Buf("ctmp")
    accP = alloc([12, 128]); baccP = Buf("accP")
    ctmpP = alloc([12, 128]); bctmpP = Buf("ctmpP")
    te = alloc([32]); bte = Buf("te")
    wte = alloc([32]); bwte = Buf("wte")
    dcy = alloc([32]); bdcy = Buf("dcy")
    Btok = alloc([2, 128], BF16); bBtok = Buf("Btok")
    top1 = top[0]
    top[0] = m_alias
    Rb = [[alloc([8, 128]) for _ in range(2)] for _ in range(2)]
    bRb = [[Buf("R%d%d" % (d, i)) for i in range(2)] for d in range(2)]
    Eb = alloc([2, 512], BF16); bEb = [Buf("Eb0"), Buf("Eb1")]
    y1 = alloc([1024]); by1 = Buf("y1")
    ytmp = alloc([1024]); bytmp = Buf("ytmp")
    ssq = alloc([8]); bssq = Buf("ssq")
    ssdo = alloc([1024], BF16); bssdo = Buf("ssdo")
    vhat = alloc([1024], BF16); bvh = Buf("vhat")
    mixc = alloc([16, 128], BF16); bmixc = Buf("mixc")
    ut = [alloc([8, 128]) for _ in range(3)]; but = [Buf("ut%d" % i) for i in range(3)]
    vt = [alloc([1024]) for _ in range(3)]; bvt = [Buf("vt%d" % i) for i in range(3)]
    zt = [alloc([1024]) for _ in range(3)]; bzt = [Buf("zt%d" % i) for i in range(3)]
    xbc3 = xbc + [alloc([12, 128], BF16)]; bxbc3 = bxbc + [Buf("xbc2")]
    top[0] = max(top[0], top1)
    tt2 = ytmp.rearrange("p (h q) -> p h q", q=128)

    v3 = lambda ap: ap.rearrange("p (h d) -> p h d", d=64)
    bc16 = lambda ap: ap.unsqueeze(2).to_broadcast([128, 16, 64])

    P.op("sp", lambda e: e.dma_start(out=MX, in_=c["mx_d"]), writes=[bMX], dsem="ldm")
    P.op("dve", lambda e: e.tensor_copy(out=wsT.rearrange("p h q -> p (h q)"), in_=wsT_f), reads=[bMX], writes=[bws])
    for half in range(2):
        P.op("pe", lambda e, half=half: e.matmul(banks[half][:, :], lhsT=ones_f,
                                                 rhs=wsT_f[:, half * 512:(half + 1) * 512], start=True, stop=True),
             reads=[bMX, bConst], writes=[bk[half]])
    for h in range(8):
        P.op("dve", lambda e, h=h: e.scalar_tensor_tensor(
            out=Bc[:, h, :], in0=banks[h // 4][:, (h % 4) * 128:(h % 4 + 1) * 128], scalar=gnb[:, h:h + 1],
            in1=bsb[:, h, :], op0=ALU.mult, op1=ALU.add), reads=[bk[h // 4], bMX, bSP], writes=[bBc])
    P.barrier()

    def dtprep(bi):
        P.op("sp", lambda e: e.dma_start(out=dtraw[:, 0:16, :], in_=c["dts"][bi].rearrange("(c p) k -> p c k", p=128)),
             writes=[bdtraw], dsem="mx2")
        P.op("sp", lambda e: e.dma_start(out=dtraw[:, 16:18, :], in_=c["dtc"][bi].rearrange("(c p) k -> p c k", p=128)),
             writes=[bdtraw], dsem="mx2")
        P.op("dve", lambda e: e.tensor_tensor(out=dtall, in0=dtraw, in1=dtb.unsqueeze(1).to_broadcast([128, 18, 32]),
                                              op=ALU.add), reads=[bdtraw, bSP], writes=[bdtall])
        P.op("act", lambda e: e.activation(out=dtall, in_=dtall, func=AF.Exp), reads=[bdtall], writes=[bdtall])
        P.op("act", lambda e: e.activation(out=dtall, in_=dtall, func=AF.Ln, bias=1.0), reads=[bdtall], writes=[bdtall])
        P.op("dve", lambda e: e.tensor_tensor(out=aall, in0=dtall, in1=Ab.unsqueeze(1).to_broadcast([128, 18, 32]),
                                              op=ALU.mult), reads=[bdtall, bConst], writes=[baall])

    def cum_and_xs(s, slot, xb, bxb):
        av = aall[:, slot, :]
        P.op("pe", lambda e: e.matmul(banks[0][:, 0:16], lhsT=triL, rhs=av[:, 0:16], start=True, stop=True),
             reads=[baall, bSP], writes=[bk[0]])
        P.op("pe", lambda e: e.matmul(banks[0][:, 16:32], lhsT=triU, rhs=av[:, 16:32], start=True, stop=True),
             reads=[baall, bSP], writes=[bk[0]])
        P.op("pe", lambda e: e.matmul(banks[0][:, 32:64], lhsT=ones_f, rhs=av, start=True, stop=True),
             reads=[baall, bConst], writes=[bk[0]])
        P.op("dve", lambda e: e.tensor_copy(out=acum[s], in_=banks[0][:, 0:64]), reads=[bk[0]], writes=[bacum[s]])
        xsT_ps = bank_bf(1)
        for j in range(8):
            P.op("pe", lambda e, j=j: e.transpose(xsT_ps[:, j * 128:(j + 1) * 128], xb[:, j, :], ident_bf),
                 reads=[bxb, bConst], writes=[bk[1]])
        P.op("act", lambda e: e.activation(out=xstok[s], in_=xsT_ps[:, 0:1024], func=AF.Copy),
             reads=[bk[1]], writes=[bxs[s]])

    def p1load(i, src_x):
        P.op("sp", lambda e: e.dma_start(out=xp[i % 2], in_=src_x), writes=[bxp[i % 2]], dsem=("mx1p", i % 2))

    def p1A(s, xpi, bxpi):
        cwb = lambda w: cw[:, :, w].unsqueeze(2).to_broadcast([128, 12, 128])
        P.op("dve", lambda e: e.tensor_tensor(out=acc, in0=xpi[:, :, 0:128], in1=cwb(0), op=ALU.mult),
             reads=[bxpi, bSP], writes=[bacc])
        P.op("pool", lambda e: e.tensor_tensor(out=accP, in0=xpi[:, :, 3:131], in1=cwb(3), op=ALU.mult),
             reads=[bxpi, bSP], writes=[baccP])
        yield
        P.op("dve", lambda e: e.tensor_tensor(out=ctmp, in0=xpi[:, :, 1:129], in1=cwb(1), op=ALU.mult),
             reads=[bxpi, bSP], writes=[bctmp])
        P.op("pool", lambda e: e.tensor_tensor(out=ctmpP, in0=xpi[:, :, 4:132], in1=cwb(4), op=ALU.mult),
             reads=[bxpi, bSP], writes=[bctmpP])
        yield
        P.op("dve", lambda e: e.tensor_tensor(out=acc, in0=acc, in1=ctmp, op=ALU.add), reads=[bacc, bctmp], writes=[bacc])
        P.op("pool", lambda e: e.tensor_tensor(out=accP, in0=accP, in1=ctmpP, op=ALU.add),
             reads=[baccP, bctmpP], writes=[baccP])
        yield
        P.op("dve", lambda e: e.tensor_tensor(out=ctmp, in0=xpi[:, :, 2:130], in1=cwb(2), op=ALU.mult),
             reads=[bxpi, bSP], writes=[bctmp])
        yield
        P.op("dve", lambda e: e.tensor_tensor(out=acc, in0=acc, in1=ctmp, op=ALU.add), reads=[bacc, bctmp], writes=[bacc])
        yield
        P.op("dve", lambda e: e.tensor_tensor(out=acc, in0=acc, in1=accP, op=ALU.add), reads=[bacc, baccP], writes=[bacc])
        for cc in range(12):
            P.op("act", lambda e, cc=cc: e.activation(out=xbc[s][:, cc, :], in_=acc[:, cc, :], func=AF.Silu,
                                                      bias=cb[:, cc:cc + 1]), reads=[bacc, bSP], writes=[bxbc[s]])
        yield

    deferred = []

    def flush():
        while deferred:
            deferred.pop(0)()

    def p1B(s, slot, fslot, bslot, store_dst):
        flush()
        cum_and_xs(s, slot, xbc[s], bxbc[s])
        yield
        P.op("dve", lambda e: e.tensor_tensor(out=te, in0=acum[s][:, 32:64], in1=acum[s][:, 0:32], op=ALU.subtract),
             reads=[bacum[s]], writes=[bte])
        P.op("act", lambda e: e.activation(out=te, in_=te, func=AF.Exp), reads=[bte], writes=[bte])
        P.op("act", lambda e: e.activation(out=dcy, in_=acum[s][:, 32:64], func=AF.Exp), reads=[bacum[s]], writes=[bdcy])
        P.op("dve", lambda e: e.tensor_tensor(out=wte, in0=dtall[:, slot, :], in1=te, op=ALU.mult),
             reads=[bdtall, bte], writes=[bwte])
        Bps = bank_bf(2)
        for g in range(2):
            P.op("pe", lambda e, g=g: e.transpose(Bps[:, g * 128:(g + 1) * 128], xbc[s][:, 8 + g, :], ident_bf),
                 reads=[bxbc[s], bConst], writes=[bk[2]])
        P.op("dve", lambda e: e.tensor_copy(out=Btok.rearrange("p g n -> p (g n)"), in_=Bps[:, 0:256]),
             reads=[bk[2]], writes=[bBtok])
        yield
        P.op("dve", lambda e: e.tensor_tensor(out=v3(xdtf[s]), in0=v3(xstok[s]), in1=bc16(wte[:, 0:16]), op=ALU.mult),
             reads=[bxs[s], bwte], writes=[bxdf[s]])
        P.op("pool", lambda e: e.tensor_tensor(out=v3(xdtb[s]), in0=v3(xstok[s]), in1=bc16(wte[:, 16:32]), op=ALU.mult),
             reads=[bxs[s], bwte], writes=[bxdb[s]])
        yield
        def smm(xd, bxd):
            for g in range(2):
                P.op("pe", lambda e, g=g, xd=xd: e.matmul(banks[3 + g][:, :], lhsT=Btok[:, g, :],
                                                         rhs=xd[:, g * 512:(g + 1) * 512], start=True, stop=True),
                     reads=[bBtok, bxd], writes=[bk[3 + g]])
        smm(xdtf[s], bxdf[s])
        yield
        if fslot is not None:
            P.op("act", lambda e: e.activation(out=prevF[:, fslot, :], in_=H[0], func=AF.Copy),
                 reads=[bH[0]], writes=[bpF[fslot]])
        P.op("pool", lambda e: e.tensor_tensor(out=v3(htmp), in0=v3(H[0]), in1=bc16(dcy[:, 0:16]), op=ALU.mult),
             reads=[bH[0], bdcy], writes=[bht])
        yield
        for g in range(2):
            P.op("dve", lambda e, g=g: e.tensor_tensor(out=H[0][:, g * 512:(g + 1) * 512],
                                                       in0=htmp[:, g * 512:(g + 1) * 512], in1=banks[3 + g][:, :],
                                                       op=ALU.add), reads=[bht, bk[3 + g]], writes=[bH[0]])
        smm(xdtb[s], bxdb[s])
        yield
        for g in range(2):
            P.op("act", lambda e, g=g: e.activation(out=prevB[:, bslot, g * 512:(g + 1) * 512], in_=banks[3 + g][:, :],
                                                    func=AF.Copy), reads=[bk[3 + g]], writes=[bpB[bslot]])
        P.op("dve", lambda e: e.tensor_copy(out=dcyB[:, bslot, :], in_=dcy[:, 16:32]), reads=[bdcy], writes=[bdB[bslot]])
        if store_dst is not None:
            deferred.append(lambda: P.op("sp", lambda e: e.dma_start(out=store_dst, in_=xbc[s]), reads=[bxbc[s]],
                                         dsem=("mxs", s)))
        yield

    def bwd_step(cur, slot, keep_prev):
        nxt = 3 - cur
        P.op("pool", lambda e: e.tensor_tensor(out=v3(htmp), in0=v3(H[cur]), in1=bc16(dcyB[:, slot, :]), op=ALU.mult),
             reads=[bH[cur], bdB[slot]], writes=[bht])
        P.op("dve", lambda e: e.tensor_tensor(out=H[nxt], in0=htmp, in1=prevB[:, slot, :], op=ALU.add),
             reads=[bht, bpB[slot]], writes=[bH[nxt]])
        if keep_prev:
            P.op("act", lambda e: e.activation(out=prevB[:, slot, :], in_=H[cur], func=AF.Copy),
                 reads=[bH[cur], bH[nxt]], writes=[bpB[slot]])
        return nxt

    def p2load(bi, ci):
        t0 = ci * 128
        l = ci % 3
        P.op("sp", lambda e: e.dma_start(out=xbc3[l], in_=c["xpost"][bi][:, :, t0:t0 + 128].rearrange("c p t -> p c t")),
             writes=[bxbc3[l]], dsem=("mx1", l))
        P.op("sp", lambda e: e.dma_start(out=ut[l], in_=c["uTs"][bi][:, :, t0:t0 + 128].rearrange("c p t -> p c t")),
             writes=[but[l]], dsem=("mx3", l))
        P.op("sp", lambda e: e.dma_start(out=vt[l], in_=c["vs"][bi][t0:t0 + 128, :]), writes=[bvt[l]], dsem=("mx4", l))
        P.op("sp", lambda e: e.dma_start(out=zt[l], in_=c["zs"][bi][t0:t0 + 128, :]), writes=[bzt[l]], dsem=("mx5", l))

    def p2A(bi, ci, s):
        l = ci % 3
        xb, bxb = xbc3[l], bxbc3[l]
        cum_and_xs(s, ci, xb, bxb)
        P.op("act", lambda e: e.activation(out=E32[s], in_=acum[s][:, 0:32], func=AF.Exp), reads=[bacum[s]], writes=[bE32[s]])
        yield
        P.op("dve", lambda e: e.tensor_tensor(out=v3(xdtf[s]), in0=v3(xstok[s]), in1=bc16(dtall[:, ci, 0:16]), op=ALU.mult),
             reads=[bxs[s], bdtall], writes=[bxdf[s]])
        P.op("pool", lambda e: e.tensor_tensor(out=v3(xdtb[s]), in0=v3(xstok[s]), in1=bc16(dtall[:, ci, 16:32]), op=ALU.mult),
             reads=[bxs[s], bdtall], writes=[bxdb[s]])
        P.op("pool", lambda e: e.tensor_tensor(out=v3(xsd[s]), in0=v3(xstok[s]), in1=bc16(dsk), op=ALU.mult),
             reads=[bxs[s], bSP], writes=[bxsd[s]])
        for g in range(2):
            P.op("pe", lambda e, g=g: e.matmul(banks[0][:, 128 + g * 128:256 + g * 128], lhsT=xb[:, 8 + g, :],
                                               rhs=xb[:, 10 + g, :], start=True, stop=True),
                 reads=[bxb], writes=[bk[0]])
        for d, msk in enumerate((triL, triU)):
            P.op("dve", lambda e, d=d, msk=msk: e.tensor_tensor(
                out=cbTm[s][:, d], in0=banks[0][:, 128:384].rearrange("p (g q) -> p g q", q=128),
                in1=msk.unsqueeze(1).to_broadcast([128, 2, 128]), op=ALU.mult),
                reads=[bk[0], bSP], writes=[bcbT[s]])
        yield
        it = 0
        for d in range(2):
            msk, tri = (mgt, triL) if d == 0 else (mlt, triU)
            eng = "dve" if d == 0 else "pool"
            for half in range(2):
                R, bR = Rb[d][half], bRb[d][half]
                P.op(eng, lambda e, d=d, half=half, tri=tri, R=R: e.tensor_tensor(
                    out=R, in0=tri.unsqueeze(1).to_broadcast([128, 8, 128]),
                    in1=aall[:, ci, d * 16 + half * 8:d * 16 + half * 8 + 8].unsqueeze(2).to_broadcast([128, 8, 128]),
                    op=ALU.mult), reads=[baall, bSP], writes=[bR])
                for q4 in range(2):
                    bn = 2 + it % 2
                    P.op("pe", lambda e, bn=bn, q4=q4, msk=msk, R=R: e.matmul(
                        banks[bn][:, :], lhsT=msk, rhs=R[:, q4 * 4:(q4 + 1) * 4, :].rearrange("p h q -> p (h q)"),
                        start=True, stop=True), reads=[bR, bSP], writes=[bk[bn]])
                    eb = it % 2
                    P.op("act", lambda e, eb=eb, bn=bn: e.activation(out=Eb[:, eb, :], in_=banks[bn][:, :], func=AF.Exp),
                         reads=[bk[bn]], writes=[bEb[eb]])
                    mi = d * 4 + half * 2 + q4
                    P.op(eng, lambda e, eb=eb, mi=mi, d=d, half=half: e.tensor_tensor(
                        out=MT[s][:, mi * 4:(mi + 1) * 4, :], in0=Eb[:, eb, :].rearrange("p (h q) -> p h q", q=128),
                        in1=cbTm[s][:, d, half, :].unsqueeze(1).to_broadcast([128, 4, 128]), op=ALU.mult),
                        reads=[bEb[eb], bcbT[s]], writes=[bMT[s][mi]])
                    it += 1
                yield

    def p2B(bi, ci, s):
        t0 = ci * 128
        l = ci % 3
        flush()
        P.op("dve", lambda e: e.memset(ssq, 0.0), writes=[bssq])
        P.op("act", lambda e: e.activation(out=zt[l], in_=zt[l], func=AF.Silu), reads=[bzt[l]], writes=[bzt[l]])
        P.op("act", lambda e: e.activation(out=vt[l], in_=vt[l], func=AF.Gelu, accum_out=ssq[:, 2:3]),
             reads=[bvt[l]], writes=[bvt[l], bssq])
        P.op("act", lambda e: e.activation(out=ut[l], in_=ut[l], func=AF.Gelu), reads=[but[l]], writes=[but[l]])
        for h in range(16):
            reg = banks[4 + h // 8][:, (h % 8) * 64:(h % 8 + 1) * 64]
            P.op("pe", lambda e, reg=reg, h=h: e.matmul(reg, lhsT=ident_bf, rhs=xsd[s][:, h * 64:(h + 1) * 64],
                                                        start=True, stop=False),
                 reads=[bConst, bxsd[s]], writes=[bk[4 + h // 8]])
            P.op("pe", lambda e, reg=reg, h=h: e.matmul(reg, lhsT=MT[s][:, h, :], rhs=xdtf[s][:, h * 64:(h + 1) * 64],
                                                        start=False, stop=False),
                 reads=[bMT[s][h // 4], bxdf[s]], writes=[bk[4 + h // 8]])
            P.op("pe", lambda e, reg=reg, h=h: e.matmul(reg, lhsT=MT[s][:, 16 + h, :], rhs=xdtb[s][:, h * 64:(h + 1) * 64],
                                                        start=False, stop=True),
                 reads=[bMT[s][4 + h // 4], bxdb[s]], writes=[bk[4 + h // 8]])
        yield
        n = 0
        for d in range(2):
            prev, bprev = (prevF, bpF) if d == 0 else (prevB, bpB)
            for g in range(2):
                bn = 6 + n % 2
                P.op("pe", lambda e, g=g, bn=bn, prev=prev: e.matmul(
                    banks[bn][:, :], lhsT=xbc3[l][:, 10 + g, :], rhs=prev[:, ci, g * 512:(g + 1) * 512],
                    start=True, stop=True), reads=[bxbc3[l], bprev[ci]], writes=[bk[bn]])
                dst = y1 if d == 0 else ytmp
                bdst = by1 if d == 0 else bytmp
                P.op("dve", lambda e, g=g, bn=bn, d=d, dst=dst: e.tensor_tensor(
                    out=dst[:, g * 512:(g + 1) * 512].rearrange("p (h d) -> p h d", d=64),
                    in0=banks[bn][:, :].rearrange("p (h d) -> p h d", d=64),
                    in1=E32[s][:, d * 16 + g * 8:d * 16 + g * 8 + 8].unsqueeze(2).to_broadcast([128, 8, 64]),
                    op=ALU.mult), reads=[bk[bn], bE32[s]], writes=[bdst])
                n += 1
            yield
        P.op("dve", lambda e: e.tensor_tensor(out=y1, in0=y1, in1=ytmp, op=ALU.add), reads=[by1, bytmp], writes=[by1])
        for g in range(2):
            P.op("dve", lambda e, g=g: e.tensor_tensor(out=y1[:, g * 512:(g + 1) * 512], in0=y1[:, g * 512:(g + 1) * 512],
                                                       in1=banks[4 + g][:, :], op=ALU.add),
                 reads=[by1, bk[4 + g]], writes=[by1])
        yield
        P.op("dve", lambda e: e.tensor_tensor(out=y1, in0=y1, in1=zt[l], op=ALU.mult), reads=[by1, bzt[l]], writes=[by1])
        for g in range(2):
            P.op("act", lambda e, g=g: e.activation(out=ytmp[:, g * 512:(g + 1) * 512], in_=y1[:, g * 512:(g + 1) * 512],
                                                    func=AF.Square, accum_out=ssq[:, g:g + 1]),
                 reads=[by1], writes=[bytmp, bssq])
        yield
        P.op("act", lambda e: e.activation(out=ytmp, in_=vt[l], func=AF.Square, accum_out=ssq[:, 3:4]),
             reads=[bvt[l]], writes=[bytmp, bssq])
        yield
        P.op("dve", lambda e: e.tensor_scalar(out=ssq[:, 4:5], in0=ssq[:, 2:3], scalar1=1.0 / 1024.0, scalar2=None,
                                              op0=ALU.mult), reads=[bssq], writes=[bssq])
        P.op("dve", lambda e: e.tensor_tensor(out=ssq[:, 5:6], in0=ssq[:, 4:5], in1=ssq[:, 4:5], op=ALU.mult),
             reads=[bssq], writes=[bssq])
        P.op("dve", lambda e: e.scalar_tensor_tensor(out=ssq[:, 2:3], in0=ssq[:, 3:4], scalar=1.0 / 1024.0,
                                                     in1=ssq[:, 5:6], op0=ALU.mult, op1=ALU.subtract),
             reads=[bssq], writes=[bssq])
        P.op("dve", lambda e: e.tensor_scalar(out=ssq[:, 0:2], in0=ssq[:, 0:2], scalar1=1.0 / 512.0, scalar2=None,
                                              op0=ALU.mult), reads=[bssq], writes=[bssq])
        P.op("act", lambda e: e.activation(out=ssq[:, 0:3], in_=ssq[:, 0:3], func=AF.Sqrt, bias=EPS, scale=1.0),
             reads=[bssq], writes=[bssq])
        P.op("dve", lambda e: e.reciprocal(out=ssq[:, 0:3], in_=ssq[:, 0:3]), reads=[bssq], writes=[bssq])
        yield
        for g in range(2):
            P.op("dve", lambda e, g=g: e.tensor_scalar(out=ssdo[:, g * 512:(g + 1) * 512], in0=y1[:, g * 512:(g + 1) * 512],
                                                       scalar1=ssq[:, g:g + 1], scalar2=None, op0=ALU.mult),
                 reads=[by1, bssq], writes=[bssdo])
        sT_bf = bank_bf(6)
        for j in range(8):
            P.op("pe", lambda e, j=j: e.transpose(sT_bf[:, j * 128:(j + 1) * 128], ssdo[:, j * 128:(j + 1) * 128], ident_bf),
                 reads=[bssdo, bConst], writes=[bk[6]])
        P.op("dve", lambda e: e.tensor_tensor(out=mixc[:, 8:16, :],
                                              in0=sT_bf[:, 0:1024].rearrange("p (j t) -> p j t", t=128),
                                              in1=ssdg.unsqueeze(2).to_broadcast([128, 8, 128]), op=ALU.mult),
             reads=[bk[6], bSP], writes=[bmixc])
        yield
        P.op("dve", lambda e: e.tensor_scalar(out=vhat, in0=vt[l], scalar1=ssq[:, 4:5], scalar2=ssq[:, 2:3],
                                              op0=ALU.subtract, op1=ALU.mult), reads=[bvt[l], bssq], writes=[bvh])
        for h in range(8):
            P.op("pe", lambda e, h=h: e.matmul(banks[4 + h // 4][:, (h % 4) * 128:(h % 4 + 1) * 128],
                                               lhsT=vhat[:, h * 128:(h + 1) * 128], rhs=wsT[:, h, :],
                                               start=True, stop=True), reads=[bvh, bws], writes=[bk[4 + h // 4]])
        yield
        for h in range(8):
            P.op("dve", lambda e, h=h: e.scalar_tensor_tensor(
                out=tt2[:, h, :], in0=banks[4 + h // 4][:, (h % 4) * 128:(h % 4 + 1) * 128], scalar=gng[:, h:h + 1],
                in1=Bc[:, h, :], op0=ALU.mult, op1=ALU.add), reads=[bk[4 + h // 4], bBc, bSP], writes=[bytmp])
        P.op("dve", lambda e: e.tensor_tensor(out=mixc[:, 0:8, :], in0=ut[l], in1=tt2, op=ALU.mult),
             reads=[but[l], bytmp], writes=[bmixc])
        deferred.append(lambda: P.op("sp", lambda e: e.dma_start(
            out=c["mixs"][bi][:, :, t0:t0 + 128].rearrange("c p t -> p c t"), in_=mixc), reads=[bmixc], dsem="mxo"))
        yield

    for bi in range(2):
        dtprep(bi)
        P.op("dve", lambda e: e.memset(H[0], 0.0), writes=[bH[0]])
        P.op("dve", lambda e: e.memset(H[1], 0.0), writes=[bH[1]])
        ch = []
        for ci in range(2):
            t0 = ci * 128
            ch.append((c["xprec"][bi][:, :, t0:t0 + 132].rearrange("c p t -> p c t"), 16 + ci, None, ci, None))
        for ci in range(NCH):
            t0 = ci * 128
            ch.append((c["xpre"][bi][:, :, t0:t0 + 132].rearrange("c p t -> p c t"), ci, ci, ci,
                       c["xpost"][bi][:, :, t0:t0 + 128].rearrange("c p t -> p c t")))
        cur = 1
        p1load(0, ch[0][0])
        for i in range(len(ch) + 1):
            if i + 1 < len(ch):
                p1load(i + 1, ch[i + 1][0])
            gA = p1A(i % 2, xp[i % 2], bxp[i % 2]) if i < len(ch) else None
            gB = None
            if i >= 1:
                src, slot, fslot, bslot, dst = ch[i - 1]
                gB = p1B((i - 1) % 2, slot, fslot, bslot, dst)
            _interleave(gA, gB)
            if i == 2:
                for cj in (1, 0):
                    cur = bwd_step(cur, cj, False)
        flush()
        for ci in range(NCH - 1, -1, -1):
            cur = bwd_step(cur, ci, True)
        P.barrier()
        p2load(bi, 0)
        for i in range(NCH + 1):
            if i + 1 < NCH:
                p2load(bi, i + 1)
            gA = p2A(bi, i, i % 2) if i < NCH else None
            gB = p2B(bi, i - 1, (i - 1) % 2) if i >= 1 else None
            _interleave(gA, gB)
        flush()
        P.barrier()


def _tile_cols(w, nch):
    return np.ascontiguousarray(w.reshape(16, 128, nch, 128).transpose(2, 1, 0, 3))


def prep_shared(inp):
    f = np.float32
    sh = {}
    sh["wmod"] = _tile_cols(np.asarray(inp["w_mod"], f)[0], 144)
    for i, k in ((1, "ffn1"), (2, "ffn2")):
        sh["wg%d" % i] = _tile_cols(np.asarray(inp[k + "_w_gate"], f)[0], NFF)
        sh["wu%d" % i] = _tile_cols(np.asarray(inp[k + "_w_up"], f)[0], NFF)
        sh["wd%d" % i] = np.ascontiguousarray(np.asarray(inp[k + "_w_down"], f)[0].reshape(NFF, 128, D))
    win = np.asarray(inp["w_in"], f)[0]
    fm = np.concatenate([win[:, 0:1024], win[:, 3072:4608]], axis=1)
    tm = win[:, 1024:3072]
    sh["winfm"] = _tile_cols(fm, 20)
    sh["wintm"] = _tile_cols(tm, 16)
    sh["windt"] = np.ascontiguousarray(win[:, 4608:4640].reshape(16, 128, 32).transpose(1, 0, 2))
    sh["wout"] = np.ascontiguousarray(np.asarray(inp["w_out"], f)[0].reshape(16, 128, D))
    mx = np.zeros((128, 2048), f)
    mx[:, 0:1024] = np.broadcast_to(np.asarray(inp["gmlp_b_s"], f)[0].reshape(1, 1024), (128, 1024))
    mx[:, 1024:2048] = np.asarray(inp["gmlp_w_s"], f)[0].transpose(2, 0, 1).reshape(128, 1024)
    sh["mx"] = mx
    return sh


def _fm(vec, n):
    return np.asarray(vec, np.float32).reshape(n, 128).T


def prep_core(inp, core):
    f = np.float32
    b0 = 2 * core
    x = np.asarray(inp["x"], f)
    ctx = np.asarray(inp["ctx"], f)
    d = {}
    d["xT"] = np.ascontiguousarray(x[b0:b0 + 2].transpose(0, 2, 1).reshape(2, FC, 128, SEQ))
    d["ctxT"] = np.ascontiguousarray(ctx[b0:b0 + 2].transpose(2, 0, 1).reshape(FC, 128, 2 * CTX))
    sp = np.zeros((128, SPW), f)
    for i, k in enumerate(("norm_ffn1", "norm_mix", "norm_ffn2")):
        sp[:, O_NG + 16 * i:O_NG + 16 * i + 16] = _fm(np.asarray(inp[k])[0], 16)
    sp[:, O_NG + 48:O_NG + 64] = _fm(np.asarray(inp["norm_final"]), 16)
    sp[:, O_BMOD:O_BMOD + 144] = _fm(np.asarray(inp["b_mod"])[0], 144)
    cwt = np.asarray(inp["conv_w"], f)[0]
    sp[:, O_CW:O_CW + 60] = cwt.T.reshape(12, 128, 5).transpose(1, 0, 2).reshape(128, 60)
    sp[:, O_CB:O_CB + 12] = _fm(np.asarray(inp["conv_b"])[0], 12)
    sp[:, O_DTB:O_DTB + 32] = np.asarray(inp["dt_bias"], f)[0].reshape(1, 32)
    sp[:, O_ALOG:O_ALOG + 32] = np.asarray(inp["a_log"], f)[0].reshape(1, 32)
    sp[:, O_DSK:O_DSK + 16] = np.asarray(inp["d_skip"], f)[0].reshape(1, 16)
    sp[:, O_SSDG:O_SSDG + 8] = _fm(np.asarray(inp["ssd_norm_g"])[0], 8)
    sp[:, O_GNG:O_GNG + 8] = _fm(np.asarray(inp["gmlp_norm_g"])[0], 8)
    sp[:, O_GNB:O_GNB + 8] = _fm(np.asarray(inp["gmlp_norm_b"])[0], 8)
    c = np.asarray(inp["c"], f)
    rows = np.stack([c[b0], c[b0 + 1], np.asarray(inp["c_ctx"], f)], axis=1)
    sp[:, O_CT:O_CT + 48] = rows.reshape(16, 128, 3).transpose(1, 0, 2).reshape(128, 48)
    ar = np.arange(128)
    ident = np.eye(128, dtype=f)
    triL = (ar[:, None] <= ar[None, :]).astype(f)
    triU = (ar[:, None] >= ar[None, :]).astype(f)
    mgt = (ar[:, None] > ar[None, :]).astype(f)
    mlt = (ar[:, None] < ar[None, :]).astype(f)
    sp[:, O_C128:O_C128 + 640] = np.stack([ident, triL, triU, mgt, mlt], axis=1).reshape(128, 640)
    d["sp"] = sp
    return d


_NC_CACHE = {}


def kernel(**inputs):
    n = 8
    if "nc" not in _NC_CACHE:
        _NC_CACHE["nc"] = build_program()
    nc = _NC_CACHE["nc"]
    sh = prep_shared(inputs)
    in_maps = []
    for core in range(n):
        m = dict(sh)
        m.update(prep_core(inputs, core))
        in_maps.append(m)
    res = run_bass_kernel_spmd(nc, in_maps, core_ids=list(range(n)))
    out = np.empty((16, SEQ, D), np.float32)
    for core in range(n):
        o = np.asarray(res.results[core]["outT"]).reshape(2, D, SEQ)
        out[2 * core:2 * core + 2] = o.transpose(0, 2, 1)
    return out
```

```python
import numpy as np
from contextlib import ExitStack
import concourse.bass as bass
import concourse.mybir as mybir
from concourse.bass_utils import run_bass_kernel_spmd

F32 = mybir.dt.float32
BF16 = mybir.dt.bfloat16
AF = mybir.ActivationFunctionType
ALU = mybir.AluOpType

D = 2048
FC = 16
DFF = 5504
NFF = 43
SEQ = 2048
CTX = 256
NCH = SEQ // 128
T = 512
EPS = 1e-6
NEG = -30000.0
GC = 4
NSLOT = 16

O_NG = 0
O_BMOD = 64
O_CW = 208
O_CB = 268
O_DTB = 280
O_ALOG = 312
O_DSK = 344
O_SSDG = 360
O_GNG = 368
O_GNB = 376
O_CT = 384
O_C128 = 432
SPW = 432 + 640


class Buf:
    __slots__ = ("name", "w", "r", "rd")

    def __init__(self, name=""):
        self.name = name
        self.w = None
        self.r = {}
        self.rd = []


class Ins:
    __slots__ = ("eng", "fn", "deps", "signal", "val", "dsem")


class Prog:
    ENG = ("pe", "act", "dve", "pool", "sp")

    def __init__(self):
        self.streams = {e: [] for e in self.ENG}
        self.dma_count = {}
        self.last_dma = {}

    def _dep(self, ins, p, kind):
        if p is ins:
            return
        pd = p.dsem is not None
        idm = ins.dsem is not None
        if not pd and not idm and p.eng == ins.eng:
            if kind != "RAW" or ins.eng == "pe":
                return
        if p not in ins.deps:
            ins.deps.append(p)
        if not pd:
            p.signal = True

    def op(self, eng, fn, reads=(), writes=(), dsem=None):
        ins = Ins()
        ins.eng = eng
        ins.fn = fn
        ins.deps = []
        ins.signal = False
        ins.val = None
        ins.dsem = dsem
        if dsem is not None:
            self.dma_count[dsem] = self.dma_count.get(dsem, 0) + 16
            ins.val = self.dma_count[dsem]
            self.last_dma[dsem] = ins
        for b in reads:
            if b.w is not None:
                self._dep(ins, b.w, "RAW")
        for b in writes:
            if b.w is not None:
                self._dep(ins, b.w, "WAW")
            for r in b.r.values():
                self._dep(ins, r, "WAR")
            for r in b.rd:
                self._dep(ins, r, "WAR")
        for b in reads:
            if dsem is not None:
                b.rd.append(ins)
            else:
                b.r[eng] = ins
        for b in writes:
            b.w = ins
            b.r = {}
            b.rd = []
        self.streams[eng].append(ins)
        return ins

    def barrier(self):
        lasts = []
        for e in self.ENG:
            for ins in reversed(self.streams[e]):
                if ins.dsem is None and ins.fn is not None:
                    ins.signal = True
                    lasts.append(ins)
                    break
        lasts += list(self.last_dma.values())
        for e in self.ENG:
            ins = Ins()
            ins.eng = e
            ins.fn = None
            ins.deps = list(lasts)
            ins.signal = False
            ins.val = None
            ins.dsem = None
            self.streams[e].append(ins)

    def emit(self, nc):
        with ExitStack() as es:
            esem = {e: es.enter_context(nc.semaphore("s_" + e)) for e in self.ENG}
            dsem = {k: es.enter_context(nc.semaphore("d_%d" % i))
                    for i, k in enumerate(self.dma_count)}
            for e in self.ENG:
                cnt = 0
                for ins in self.streams[e]:
                    if ins.dsem is None and ins.signal:
                        cnt += 1
                        ins.val = cnt
            block = es.enter_context(nc.Block())

            def run(e, engobj):
                waited = {}
                for ins in self.streams[e]:
                    for p in ins.deps:
                        if p.dsem is not None:
                            key = ("d", p.dsem)
                            sem = dsem[p.dsem]
                        else:
                            key = ("e", p.eng)
                            sem = esem[p.eng]
                        if waited.get(key, 0) >= p.val:
                            continue
                        waited[key] = p.val
                        engobj.wait_ge(sem, p.val)
                    if ins.fn is None:
                        continue
                    h = ins.fn(engobj)
                    if ins.dsem is not None:
                        h.then_inc(dsem[ins.dsem], 16)
                    elif ins.signal:
                        h.then_inc(esem[e], 1)

            @block.tensor
            def _(t):
                run("pe", t)

            @block.scalar
            def _(s):
                run("act", s)

            @block.vector
            def _(v):
                run("dve", v)

            @block.gpsimd
            def _(g):
                run("pool", g)

            @block.sync
            def _(sp):
                run("sp", sp)


def build_program(debug=False, arena_words=51 * 1024, phases=4):
    nc = bass.Bass("TRN2", target_bir_lowering=False)
    P = Prog()

    def din(name, shape, dt=F32):
        return nc.dram_tensor(name, list(shape), dt, kind="ExternalInput").ap()

    skind = "ExternalOutput" if debug else "Internal"

    def dscr(name, shape, dt=F32):
        return nc.dram_tensor(name, list(shape), dt, kind=skind).ap()

    xT_d = din("xT", [2, FC, 128, SEQ])
    ctxT_d = din("ctxT", [FC, 128, 2 * CTX])
    sp_d = din("sp", [128, SPW])
    mx_d = din("mx", [128, 2048])
    wmod_d = din("wmod", [144, 128, 16, 128])
    wg_d = [din("wg1", [NFF, 128, 16, 128]), din("wg2", [NFF, 128, 16, 128])]
    wu_d = [din("wu1", [NFF, 128, 16, 128]), din("wu2", [NFF, 128, 16, 128])]
    wd_d = [din("wd1", [NFF, 128, D]), din("wd2", [NFF, 128, D])]
    winfm_d = din("winfm", [20, 128, 16, 128])
    wintm_d = din("wintm", [16, 128, 16, 128])
    windt_d = din("windt", [128, 16, 32])
    wout_d = din("wout", [16, 128, D])
    outT_d = nc.dram_tensor("outT", [2, FC, 128, SEQ], F32, kind="ExternalOutput").ap()

    x1s = dscr("x1s", [2, FC, 128, SEQ])
    uTs = dscr("uTs", [2, 8, 128, SEQ])
    vs = dscr("vs", [2, SEQ, 1024])
    zs = dscr("zs", [2, SEQ, 1024])
    dts = dscr("dts", [2, SEQ, 32])
    dtc = dscr("dtc", [2, CTX, 32])
    xpre = dscr("xpre", [2, 12, 128, SEQ + 4])
    xprec = dscr("xprec", [2, 12, 128, CTX + 4])
    xpost = dscr("xpost", [2, 12, 128, SEQ], BF16)
    mixs = dscr("mixs", [2, FC, 128, SEQ], BF16)

    with ExitStack() as es:
        arena = es.enter_context(nc.sbuf_tensor("arena", [128, arena_words], F32))
        top = [0]

        def alloc(shape, dt=F32, parts=128):
            n = int(np.prod(shape))
            w = n if dt == F32 else (n + 1) // 2
            w = (w + 7) // 8 * 8
            o = top[0]
            top[0] += w
            assert top[0] <= arena_words, ("SBUF arena overflow", top[0], arena_words)
            ap = arena[0:parts, o:o + w]
            if dt != F32:
                ap = ap.bitcast(dt)
            ap = ap[:, 0:n]
            if len(shape) == 2:
                ap = ap.rearrange("p (a b) -> p a b", b=shape[1])
            elif len(shape) == 3:
                ap = ap.rearrange("p (a b c) -> p a b c", b=shape[1], c=shape[2])
            return ap

        banks = [es.enter_context(nc.psum_tensor("bank%d" % i, [128, 512], F32)) for i in range(8)]
        bk = [Buf("bank%d" % i) for i in range(8)]

        def bank_bf(i):
            return banks[i][:, :].bitcast(BF16)

        SPt = alloc([SPW])
        bSP = Buf("SP")
        ident_bf = alloc([128], BF16)
        ones_bf = alloc([128], BF16)
        ones_f = alloc([128], F32)
        zero_f = alloc([128], F32)
        modT = alloc([9, 16, 3])
        der = alloc([9, 16, 3])
        Ab = alloc([32])
        bConst = Buf("consts")
        bMod = Buf("mod")
        bDer = Buf("der")
        persist_top = top[0]

        ng = SPt[:, O_NG:O_NG + 64].rearrange("p (a b) -> p a b", b=16)
        bmodT = SPt[:, O_BMOD:O_BMOD + 144]
        cw = SPt[:, O_CW:O_CW + 60].rearrange("p (a b) -> p a b", b=5)
        cb = SPt[:, O_CB:O_CB + 12]
        dtb = SPt[:, O_DTB:O_DTB + 32]
        alog = SPt[:, O_ALOG:O_ALOG + 32]
        dsk = SPt[:, O_DSK:O_DSK + 16]
        ssdg = SPt[:, O_SSDG:O_SSDG + 8]
        gng = SPt[:, O_GNG:O_GNG + 8]
        gnb = SPt[:, O_GNB:O_GNB + 8]
        cT = SPt[:, O_CT:O_CT + 48].rearrange("p (a b) -> p a b", b=3)
        c128 = SPt[:, O_C128:O_C128 + 640].rearrange("p (a b) -> p a b", b=128)
        ident_f, triL, triU, mgt, mlt = (c128[:, i, :] for i in range(5))

        class Ring:
            def __init__(self, n):
                self.n = n
                self.slots = [alloc([2048], BF16) for _ in range(n)]
                self.bufs = [Buf("ws%d" % i) for i in range(n)]
                self.i = 0

            def load(self, src, kc_view):
                s = self.i % self.n
                self.i += 1
                ap = self.slots[s]
                if kc_view:
                    ap = ap.rearrange("p (a b) -> p a b", b=128)
                P.op("pool", lambda e, ap=ap, src=src: e.dma_start(out=ap, in_=src),
                     writes=[self.bufs[s]], dsem=("w", s))
                return ap, self.bufs[s]

        P.op("sp", lambda e: e.dma_start(out=SPt, in_=sp_d), writes=[bSP], dsem="ld0")
        P.op("dve", lambda e: e.tensor_copy(out=ident_bf, in_=ident_f), reads=[bSP], writes=[bConst])
        P.op("dve", lambda e: e.memset(ones_bf, 1.0 / 2048.0), writes=[bConst])
        P.op("dve", lambda e: e.memset(ones_f, 1.0), writes=[bConst])
        P.op("dve", lambda e: e.memset(zero_f, 0.0), writes=[bConst])
        P.op("act", lambda e: e.activation(out=Ab, in_=alog, func=AF.Exp), reads=[bSP], writes=[bConst])
        P.op("dve", lambda e: e.tensor_scalar(out=Ab, in0=Ab, scalar1=-1.0, scalar2=None, op0=ALU.mult),
             reads=[bConst], writes=[bConst])
        for b2 in range(2):
            for (dst, L) in ((xpre, SEQ), (xprec, CTX)):
                for o in (0, L + 2):
                    P.op("sp", lambda e, dst=dst, b2=b2, o=o: e.dma_start(
                        out=dst[b2][:, :, o:o + 2].rearrange("c p t -> p c t"),
                        in_=zero_f[:, 0:24].rearrange("p (c t) -> p c t", t=2)),
                        reads=[bConst], dsem="pad")

        m0 = top[0]
        ring = Ring(NSLOT)
        scT = alloc([16, 3], BF16)
        bsc = Buf("scT")
        P.op("act", lambda e: e.activation(out=scT, in_=cT, func=AF.Silu), reads=[bSP], writes=[bsc])
        mod_ps = banks[7]
        for ch in range(144):
            wt, wb = ring.load(wmod_d[ch], True)
            for kc in range(16):
                P.op("pe", lambda e, wt=wt, kc=kc, ch=ch: e.matmul(
                    mod_ps[:, ch * 3:(ch + 1) * 3], lhsT=wt[:, kc, :], rhs=scT[:, kc, :],
                    start=(kc == 0), stop=(kc == 15)), reads=[wb, bsc], writes=[bk[7]])
        modT_f = modT.rearrange("p a b c -> p (a b) c")
        P.op("dve", lambda e: e.tensor_tensor(
            out=modT_f, in0=mod_ps[:, 0:432].rearrange("p (a c) -> p a c", c=3),
            in1=bmodT.unsqueeze(2).to_broadcast([128, 144, 3]), op=ALU.add),
            reads=[bk[7], bSP], writes=[bMod])
        for m in range(9):
            if m in (1, 4, 7):
                k = m // 3
                P.op("dve", lambda e, m=m: e.tensor_scalar(out=der[:, m], in0=modT[:, m], scalar1=1.0,
                                                           scalar2=None, op0=ALU.add),
                     reads=[bMod], writes=[bDer])
                P.op("dve", lambda e, m=m, k=k: e.tensor_tensor(
                    out=der[:, m], in0=der[:, m],
                    in1=ng[:, k, :].unsqueeze(2).to_broadcast([128, 16, 3]), op=ALU.mult),
                    reads=[bDer, bSP], writes=[bDer])
            elif m in (2, 8):
                P.op("dve", lambda e, m=m: e.tensor_scalar(out=der[:, m], in0=modT[:, m], scalar1=0.5,
                                                           scalar2=None, op0=ALU.mult),
                     reads=[bMod], writes=[bDer])
            else:
                P.op("dve", lambda e, m=m: e.tensor_copy(out=der[:, m], in_=modT[:, m]),
                     reads=[bMod], writes=[bDer])

        def sc(m, fc, r):
            return der[:, m, fc, r:r + 1]

        xT2 = [alloc([16, T]) for _ in range(2)]
        bx2 = [[Buf("x%d_%d" % (j, i)) for i in range(16)] for j in range(2)]
        cur = {"x": xT2[0], "bx": bx2[0]}
        hT = alloc([16, T], BF16)
        bh = [Buf("h%d" % i) for i in range(16)]
        AT = alloc([2 * GC, T], BF16)
        bA = [Buf("A%d" % i) for i in range(2 * GC)]
        sq = alloc([4, T], BF16)
        bsq = [Buf("sq%d" % i) for i in range(4)]
        tt = alloc([2, T])
        btt = [Buf("tt0"), Buf("tt1")]
        sg = alloc([2, T])
        bsg = [Buf("sg0"), Buf("sg1")]
        rstd = alloc([T])
        brs = Buf("rstd")
        stg = alloc([4, 512])
        bst = [Buf("st%d" % i) for i in range(4)]
        wdt = alloc([16, 32], BF16)
        bwdt = Buf("wdt")
        mixT = alloc([16, T], BF16)
        bmx = [Buf("mx%d" % i) for i in range(16)]
        ffn_top = top[0]
        stg_i = [0]
        P.op("pool", lambda e: e.dma_start(out=wdt, in_=windt_d), writes=[bwdt], dsem="wdt")

        def norm_prep(k, r, mg, ms):
            xTt, bx = cur["x"], cur["bx"]
            for fc in range(16):
                s = fc % 4
                if fc % 2 == 0:
                    P.op("act", lambda e, fc=fc, s=s: e.activation(out=sq[:, s, :], in_=xTt[:, fc, :], func=AF.Square),
                         reads=[bx[fc]], writes=[bsq[s]])
                else:
                    P.op("dve", lambda e, fc=fc, s=s: e.tensor_tensor(out=sq[:, s, :], in0=xTt[:, fc, :],
                                                                      in1=xTt[:, fc, :], op=ALU.mult),
                         reads=[bx[fc]], writes=[bsq[s]])
                P.op("pe", lambda e, fc=fc, s=s: e.matmul(banks[6][:, 0:T], lhsT=ones_bf, rhs=sq[:, s, :],
                                                          start=(fc == 0), stop=(fc == 15)),
                     reads=[bsq[s], bConst], writes=[bk[6]])
            P.op("act", lambda e: e.activation(out=rstd, in_=banks[6][:, 0:T], func=AF.Sqrt, bias=EPS, scale=1.0),
                 reads=[bk[6]], writes=[brs])
            P.op("dve", lambda e: e.reciprocal(out=rstd, in_=rstd), reads=[brs], writes=[brs])
            if mg is None:
                return
            for fc in range(16):
                s = fc % 2
                P.op("dve", lambda e, fc=fc, s=s: e.tensor_tensor(out=tt[:, s, :], in0=xTt[:, fc, :], in1=rstd,
                                                                  op=ALU.mult),
                     reads=[bx[fc], brs], writes=[btt[s]])
                P.op("act", lambda e, fc=fc, s=s: e.activation(out=hT[:, fc, :], in_=tt[:, s, :], func=AF.Identity,
                                                               scale=sc(mg, fc, r), bias=sc(ms, fc, r)),
                     reads=[btt[s], bDer], writes=[bh[fc]])

        def passB(chunks, rhs_of, rbufs_of, mscale, r, wsrc):
            xTt, bx = cur["x"], cur["bx"]
            slots = [ring.load(wsrc(c), False) for c in chunks]
            for dc in range(16):
                fb = 4 + dc % 4
                for i, c in enumerate(chunks):
                    wt, wb = slots[i]
                    P.op("pe", lambda e, wt=wt, dc=dc, i=i, c=c, fb=fb, n=len(chunks): e.matmul(
                        banks[fb][:, 0:T], lhsT=wt[:, dc * 128:(dc + 1) * 128], rhs=rhs_of(c),
                        start=(i == 0), stop=(i == n - 1)),
                        reads=[wb] + rbufs_of(c), writes=[bk[fb]])
                P.op("dve", lambda e, dc=dc, fb=fb: e.scalar_tensor_tensor(
                    out=xTt[:, dc, :], in0=banks[fb][:, 0:T], scalar=sc(mscale, dc, r), in1=xTt[:, dc, :],
                    op0=ALU.mult, op1=ALU.add),
                    reads=[bk[fb], bx[dc], bDer], writes=[bx[dc]])

        def ffn(k, r):
            wi = 0 if k == 0 else 1
            norm_prep(k, r, 3 * k + 1, 3 * k)
            groups = [list(range(g, min(g + GC, NFF))) for g in range(0, NFF, GC)]

            def passA(gi):
                for i, c in enumerate(groups[gi]):
                    a = (gi % 2) * GC + i
                    pb = c % 2
                    wgt, wgb = ring.load(wg_d[wi][c], True)
                    wut, wub = ring.load(wu_d[wi][c], True)
                    for kc in range(16):
                        P.op("pe", lambda e, wgt=wgt, kc=kc, pb=pb: e.matmul(
                            banks[pb][:, 0:T], lhsT=wgt[:, kc, :], rhs=hT[:, kc, :],
                            start=(kc == 0), stop=(kc == 15)), reads=[wgb, bh[kc]], writes=[bk[pb]])
                    for kc in range(16):
                        P.op("pe", lambda e, wut=wut, kc=kc, pb=pb: e.matmul(
                            banks[2 + pb][:, 0:T], lhsT=wut[:, kc, :], rhs=hT[:, kc, :],
                            start=(kc == 0), stop=(kc == 15)), reads=[wub, bh[kc]], writes=[bk[2 + pb]])
                    P.op("act", lambda e, pb=pb: e.activation(out=sg[:, pb, :], in_=banks[pb][:, 0:T], func=AF.Silu),
                         reads=[bk[pb]], writes=[bsg[pb]])
                    P.op("dve", lambda e, pb=pb, a=a: e.tensor_tensor(out=AT[:, a, :], in0=sg[:, pb, :],
                                                                      in1=banks[2 + pb][:, 0:T], op=ALU.mult),
                         reads=[bsg[pb], bk[2 + pb]], writes=[bA[a]])

            def doB(gi):
                base = (gi % 2) * GC
                g0 = groups[gi][0]
                passB(groups[gi], lambda c: AT[:, base + (c - g0), :], lambda c: [bA[base + (c - g0)]],
                      3 * k + 2, r, lambda c: wd_d[wi][c])

            passA(0)
            for gi in range(1, len(groups)):
                passA(gi)
                doB(gi - 1)
            doB(len(groups) - 1)

        def stage():
            i = stg_i[0] % 4
            stg_i[0] += 1
            return stg[:, i, :], (bst[i], ("st", i))

        def inproj(r, is_ctx, bi, t0):
            norm_prep(1, r, 4, 3)
            chs = list(range(8, 20)) if is_ctx else list(range(20))
            for n, ch in enumerate(chs):
                wt, wb = ring.load(winfm_d[ch], True)
                fb = 4 + n % 2
                for kc in range(16):
                    P.op("pe", lambda e, wt=wt, kc=kc, fb=fb: e.matmul(
                        banks[fb][:, 0:T], lhsT=wt[:, kc, :], rhs=hT[:, kc, :],
                        start=(kc == 0), stop=(kc == 15)), reads=[wb, bh[kc]], writes=[bk[fb]])
                st, sb = stage()
                if ch < 8:
                    P.op("act", lambda e, st=st, fb=fb: e.activation(out=st, in_=banks[fb][:, 0:T], func=AF.Gelu),
                         reads=[bk[fb]], writes=[sb[0]])
                elif n % 2 == 0:
                    P.op("act", lambda e, st=st, fb=fb: e.activation(out=st, in_=banks[fb][:, 0:T], func=AF.Copy),
                         reads=[bk[fb]], writes=[sb[0]])
                else:
                    P.op("dve", lambda e, st=st, fb=fb: e.tensor_copy(out=st, in_=banks[fb][:, 0:T]),
                         reads=[bk[fb]], writes=[sb[0]])
                if ch < 8:
                    P.op("sp", lambda e, st=st, ch=ch: e.dma_start(out=uTs[bi][ch][:, t0:t0 + T], in_=st),
                         reads=[sb[0]], dsem=sb[1])
                elif not is_ctx:
                    P.op("sp", lambda e, st=st, ch=ch: e.dma_start(out=xpre[bi][ch - 8][:, 2 + t0:2 + t0 + T], in_=st),
                         reads=[sb[0]], dsem=sb[1])
                else:
                    for s2 in range(2):
                        P.op("sp", lambda e, st=st, ch=ch, s2=s2: e.dma_start(
                            out=xprec[s2][ch - 8][:, 2:2 + CTX], in_=st[:, s2 * CTX:(s2 + 1) * CTX]),
                            reads=[sb[0]], dsem=sb[1])
            if not is_ctx:
                for ch in range(16):
                    wt, wb = ring.load(wintm_d[ch], True)
                    fb = 4 + ch % 2
                    for m in range(T // 128):
                        for kc in range(16):
                            P.op("pe", lambda e, wt=wt, kc=kc, fb=fb, m=m: e.matmul(
                                banks[fb][:, m * 128:(m + 1) * 128], lhsT=hT[:, kc, m * 128:(m + 1) * 128],
                                rhs=wt[:, kc, :], start=(kc == 0), stop=(kc == 15)),
                                reads=[wb, bh[kc]], writes=[bk[fb]])
                    st, sb = stage()
                    fn = AF.Gelu if ch < 8 else AF.Silu
                    P.op("act", lambda e, st=st, fb=fb, fn=fn: e.activation(out=st, in_=banks[fb][:, :], func=fn),
                         reads=[bk[fb]], writes=[sb[0]])
                    dst = vs if ch < 8 else zs
                    c0 = (ch % 8) * 128
                    P.op("sp", lambda e, st=st, dst=dst, c0=c0: e.dma_start(
                        out=dst[bi][t0:t0 + T, c0:c0 + 128].rearrange("(m p) c -> p m c", p=128),
                        in_=st.rearrange("p (m c) -> p m c", c=128)), reads=[sb[0]], dsem=sb[1])
            nm = T // 128
            for m in range(nm):
                for kc in range(16):
                    P.op("pe", lambda e, kc=kc, m=m: e.matmul(
                        banks[7][:, m * 32:(m + 1) * 32], lhsT=hT[:, kc, m * 128:(m + 1) * 128],
                        rhs=wdt[:, kc, :], start=(kc == 0), stop=(kc == 15)),
                        reads=[bwdt, bh[kc]], writes=[bk[7]])
            st, sb = stage()
            P.op("dve", lambda e, st=st: e.tensor_copy(out=st[:, 0:nm * 32], in_=banks[7][:, 0:nm * 32]),
                 reads=[bk[7]], writes=[sb[0]])
            if not is_ctx:
                P.op("sp", lambda e, st=st: e.dma_start(
                    out=dts[bi][t0:t0 + T, :].rearrange("(m p) c -> p m c", p=128),
                    in_=st[:, 0:nm * 32].rearrange("p (m c) -> p m c", c=32)), reads=[sb[0]], dsem=sb[1])
            else:
                for s2 in range(2):
                    P.op("sp", lambda e, st=st, s2=s2: e.dma_start(
                        out=dtc[s2].rearrange("(m p) c -> p m c", p=128),
                        in_=st[:, s2 * 64:(s2 + 1) * 64].rearrange("p (m c) -> p m c", c=32)),
                        reads=[sb[0]], dsem=sb[1])

        tiles = [("ctx", 0, 0)] + [("lat", bi, t0) for bi in range(2) for t0 in range(0, SEQ, T)]
        if phases >= 1:
            def load1(i):
                kind, bi, t0 = tiles[i]
                if kind == "ctx":
                    src = ctxT_d.rearrange("c p t -> p c t")
                else:
                    src = xT_d[bi][:, :, t0:t0 + T].rearrange("c p t -> p c t")
                P.op("sp", lambda e, src=src, i=i: e.dma_start(out=xT2[i % 2], in_=src), writes=bx2[i % 2],
                     dsem=("xl", i % 2))
            load1(0)
            for i, (kind, bi, t0) in enumerate(tiles):
                r = 2 if kind == "ctx" else bi
                if i + 1 < len(tiles):
                    load1(i + 1)
                cur["x"], cur["bx"] = xT2[i % 2], bx2[i % 2]
                ffn(0, r)
                inproj(r, kind == "ctx", bi, t0)
                if kind == "lat":
                    P.op("sp", lambda e, bi=bi, t0=t0, i=i: e.dma_start(
                        out=x1s[bi][:, :, t0:t0 + T].rearrange("c p t -> p c t"), in_=xT2[i % 2]),
                        reads=bx2[i % 2], dsem=("xs", i % 2))
        P.barrier()

        if phases >= 2:
            top[0] = persist_top
            mixer(nc, P, alloc, banks, bk, bank_bf, dict(
                SPt=SPt, bSP=bSP, bConst=bConst, ident_bf=ident_bf, top=top,
                ones_f=ones_f, zero_f=zero_f, ident_f=ident_f, Ab=Ab, cw=cw, cb=cb, dtb=dtb, dsk=dsk, ssdg=ssdg, gng=gng,
                gnb=gnb, triL=triL, triU=triU, mgt=mgt, mlt=mlt, mx_d=mx_d, uTs=uTs, vs=vs, zs=zs, dts=dts,
                dtc=dtc, xpre=xpre, xprec=xprec, xpost=xpost, mixs=mixs))
            P.barrier()
            top[0] = ffn_top

        if phases >= 3:
            t3 = [(bi, t0) for bi in range(2) for t0 in range(0, SEQ, T)]

            def load3x(i):
                bi, t0 = t3[i]
                P.op("sp", lambda e, bi=bi, t0=t0, i=i: e.dma_start(
                    out=xT2[i % 2], in_=x1s[bi][:, :, t0:t0 + T].rearrange("c p t -> p c t")),
                    writes=bx2[i % 2], dsem=("xl", i % 2))

            def load3m(i):
                bi, t0 = t3[i]
                P.op("sp", lambda e, bi=bi, t0=t0: e.dma_start(
                    out=mixT, in_=mixs[bi][:, :, t0:t0 + T].rearrange("c p t -> p c t")),
                    writes=bmx, dsem="ml")
            load3x(0)
            load3m(0)
            for i, (bi, t0) in enumerate(t3):
                if i + 1 < len(t3):
                    load3x(i + 1)
                cur["x"], cur["bx"] = xT2[i % 2], bx2[i % 2]
                xTt, bx = cur["x"], cur["bx"]
                for g in range(4):
                    passB(list(range(4 * g, 4 * g + 4)), lambda c: mixT[:, c, :], lambda c: [bmx[c]],
                          5, bi, lambda c: wout_d[c])
                if i + 1 < len(t3):
                    load3m(i + 1)
                ffn(2, bi)
                norm_prep(3, bi, None, None)
                for fc in range(16):
                    s = fc % 2
                    P.op("dve", lambda e, fc=fc, s=s, xTt=xTt: e.tensor_tensor(out=tt[:, s, :], in0=xTt[:, fc, :],
                                                                               in1=rstd, op=ALU.mult),
                         reads=[bx[fc], brs], writes=[btt[s]])
                    P.op("act", lambda e, fc=fc, s=s, xTt=xTt: e.activation(out=xTt[:, fc, :], in_=tt[:, s, :],
                                                                            func=AF.Copy, scale=ng[:, 3, fc:fc + 1]),
                         reads=[btt[s], bSP], writes=[bx[fc]])
                P.op("sp", lambda e, bi=bi, t0=t0, xTt=xTt: e.dma_start(
                    out=outT_d[bi][:, :, t0:t0 + T].rearrange("c p t -> p c t"), in_=xTt),
                    reads=bx, dsem=("out", i % 2))
        P.barrier()
        P.emit(nc)
    return nc


def _interleave(*gens):
    gens = [g for g in gens if g is not None]
    while gens:
        for g in list(gens):
            try:
                next(g)
            except StopIteration:
                gens.remove(g)


def mixer(nc, P, alloc, banks, bk, bank_bf, c):
    SPt, bSP, bConst = c["SPt"], c["bSP"], c["bConst"]
    ident_bf, ones_f = c["ident_bf"], c["ones_f"]
    Ab, cw, cb, dtb, dsk, ssdg, gng, gnb = c["Ab"], c["cw"], c["cb"], c["dtb"], c["dsk"], c["ssdg"], c["gng"], c["gnb"]
    triL, triU, mgt, mlt = c["triL"], c["triU"], c["mgt"], c["mlt"]
    top = c["top"]

    wsT = alloc([8, 128], BF16); bws = Buf("wsT")
    Bc = alloc([8, 128]); bBc = Buf("Bc")
    prevF = alloc([16, 1024], BF16); bpF = [Buf("pF%d" % i) for i in range(16)]
    prevB = alloc([16, 1024], BF16); bpB = [Buf("pB%d" % i) for i in range(16)]
    dcyB = alloc([16, 16]); bdB = [Buf("dB%d" % i) for i in range(16)]
    dtraw = alloc([18, 32]); bdtraw = Buf("dtraw")
    dtall = alloc([18, 32]); bdtall = Buf("dtall")
    aall = alloc([18, 32]); baall = Buf("aall")
    xbc = [alloc([12, 128], BF16) for _ in range(2)]; bxbc = [Buf("xbc0"), Buf("xbc1")]
    acum = [alloc([64]) for _ in range(2)]; bacum = [Buf("ac0"), Buf("ac1")]
    E32 = [alloc([32]) for _ in range(2)]; bE32 = [Buf("E0"), Buf("E1")]
    xstok = [alloc([1024], BF16) for _ in range(2)]; bxs = [Buf("xs0"), Buf("xs1")]
    xdtf = [alloc([1024], BF16) for _ in range(2)]; bxdf = [Buf("xf0"), Buf("xf1")]
    xdtb = [alloc([1024], BF16) for _ in range(2)]; bxdb = [Buf("xb0"), Buf("xb1")]
    xsd = [alloc([1024], BF16) for _ in range(2)]; bxsd = [Buf("xsd0"), Buf("xsd1")]
    cbTm = [alloc([2, 2, 128], BF16) for _ in range(2)]; bcbT = [Buf("cb0"), Buf("cb1")]
    MT = [alloc([32, 128], BF16) for _ in range(2)]
    bMT = [[Buf("MT%d_%d" % (s, i)) for i in range(8)] for s in range(2)]
    m_alias = top[0]
    MX = alloc([2048]); bMX = Buf("MX")
    bsb = MX[:, 0:1024].rearrange("p (h q) -> p h q", q=128)
    wsT_f = MX[:, 1024:2048]
    top[0] = m_alias
    H = [alloc([1024]) for _ in range(3)]; bH = [Buf("H%d" % i) for i in range(3)]
    htmp = alloc([1024]); bht = Buf("htmp")
    xp = [alloc([12, 132]) for _ in range(2)]; bxp = [Buf("xp0"), Buf("xp1")]
    acc = alloc([12, 128]); bacc = Buf("acc")
    ctmp = alloc([12, 128]); bctmp = Buf("ctmp")
    accP = alloc([12, 128]); baccP = Buf("accP")
    ctmpP = alloc([12, 128]); bctmpP = Buf("ctmpP")
    te = alloc([32]); bte = Buf("te")
    wte = alloc([32]); bwte = Buf("wte")
    dcy = alloc([32]); bdcy = Buf("dcy")
    Btok = alloc([2, 128], BF16); bBtok = Buf("Btok")
    top1 = top[0]
    top[0] = m_alias
    Rb = [[alloc([8, 128]) for _ in range(2)] for _ in range(2)]
    bRb = [[Buf("R%d%d" % (d, i)) for i in range(2)] for d in range(2)]
    Eb = alloc([2, 512], BF16); bEb = [Buf("Eb0"), Buf("Eb1")]
    y1 = alloc([1024]); by1 = Buf("y1")
    ytmp = alloc([1024]); bytmp = Buf("ytmp")
    ssq = alloc([8]); bssq = Buf("ssq")
    ssdo = alloc([1024], BF16); bssdo = Buf("ssdo")
    vhat = alloc([1024], BF16); bvh = Buf("vhat")
    mixc = alloc([16, 128], BF16); bmixc = Buf("mixc")
    ut = [alloc([8, 128]) for _ in range(3)]; but = [Buf("ut%d" % i) for i in range(3)]
    vt = [alloc([1024]) for _ in range(3)]; bvt = [Buf("vt%d" % i) for i in range(3)]
    zt = [alloc([1024]) for _ in range(3)]; bzt = [Buf("zt%d" % i) for i in range(3)]
    xbc3 = xbc + [alloc([12, 128], BF16)]; bxbc3 = bxbc + [Buf("xbc2")]
    top[0] = max(top[0], top1)
    tt2 = ytmp.rearrange("p (h q) -> p h q", q=128)

    v3 = lambda ap: ap.rearrange("p (h d) -> p h d", d=64)
    bc16 = lambda ap: ap.unsqueeze(2).to_broadcast([128, 16, 64])

    P.op("sp", lambda e: e.dma_start(out=MX, in_=c["mx_d"]), writes=[bMX], dsem="ldm")
    P.op("dve", lambda e: e.tensor_copy(out=wsT.rearrange("p h q -> p (h q)"), in_=wsT_f), reads=[bMX], writes=[bws])
    for half in range(2):
        P.op("pe", lambda e, half=half: e.matmul(banks[half][:, :], lhsT=ones_f,
                                                 rhs=wsT_f[:, half * 512:(half + 1) * 512], start=True, stop=True),
             reads=[bMX, bConst], writes=[bk[half]])
    for h in range(8):
        P.op("dve", lambda e, h=h: e.scalar_tensor_tensor(
            out=Bc[:, h, :], in0=banks[h // 4][:, (h % 4) * 128:(h % 4 + 1) * 128], scalar=gnb[:, h:h + 1],
            in1=bsb[:, h, :], op0=ALU.mult, op1=ALU.add), reads=[bk[h // 4], bMX, bSP], writes=[bBc])
    P.barrier()

    def dtprep(bi):
        P.op("sp", lambda e: e.dma_start(out=dtraw[:, 0:16, :], in_=c["dts"][bi].rearrange("(c p) k -> p c k", p=128)),
             writes=[bdtraw], dsem="mx2")
        P.op("sp", lambda e: e.dma_start(out=dtraw[:, 16:18, :], in_=c["dtc"][bi].rearrange("(c p) k -> p c k", p=128)),
             writes=[bdtraw], dsem="mx2")
        P.op("dve", lambda e: e.tensor_tensor(out=dtall, in0=dtraw, in1=dtb.unsqueeze(1).to_broadcast([128, 18, 32]),
                                              op=ALU.add), reads=[bdtraw, bSP], writes=[bdtall])
        P.op("act", lambda e: e.activation(out=dtall, in_=dtall, func=AF.Exp), reads=[bdtall], writes=[bdtall])
        P.op("act", lambda e: e.activation(out=dtall, in_=dtall, func=AF.Ln, bias=1.0), reads=[bdtall], writes=[bdtall])
        P.op("dve", lambda e: e.tensor_tensor(out=aall, in0=dtall, in1=Ab.unsqueeze(1).to_broadcast([128, 18, 32]),
                                              op=ALU.mult), reads=[bdtall, bConst], writes=[baall])

    def cum_and_xs(s, slot, xb, bxb):
        av = aall[:, slot, :]
        P.op("pe", lambda e: e.matmul(banks[0][:, 0:16], lhsT=triL, rhs=av[:, 0:16], start=True, stop=True),
             reads=[baall, bSP], writes=[bk[0]])
        P.op("pe", lambda e: e.matmul(banks[0][:, 16:32], lhsT=triU, rhs=av[:, 16:32], start=True, stop=True),
             reads=[baall, bSP], writes=[bk[0]])
        P.op("pe", lambda e: e.matmul(banks[0][:, 32:64], lhsT=ones_f, rhs=av, start=True, stop=True),
             reads=[baall, bConst], writes=[bk[0]])
        P.op("dve", lambda e: e.tensor_copy(out=acum[s], in_=banks[0][:, 0:64]), reads=[bk[0]], writes=[bacum[s]])
        xsT_ps = bank_bf(1)
        for j in range(8):
            P.op("pe", lambda e, j=j: e.transpose(xsT_ps[:, j * 128:(j + 1) * 128], xb[:, j, :], ident_bf),
                 reads=[bxb, bConst], writes=[bk[1]])
        P.op("act", lambda e: e.activation(out=xstok[s], in_=xsT_ps[:, 0:1024], func=AF.Copy),
             reads=[bk[1]], writes=[bxs[s]])

    def p1load(i, src_x):
        P.op("sp", lambda e: e.dma_start(out=xp[i % 2], in_=src_x), writes=[bxp[i % 2]], dsem=("mx1p", i % 2))

    def p1A(s, xpi, bxpi):
        cwb = lambda w: cw[:, :, w].unsqueeze(2).to_broadcast([128, 12, 128])
        P.op("dve", lambda e: e.tensor_tensor(out=acc, in0=xpi[:, :, 0:128], in1=cwb(0), op=ALU.mult),
             reads=[bxpi, bSP], writes=[bacc])
        P.op("pool", lambda e: e.tensor_tensor(out=accP, in0=xpi[:, :, 3:131], in1=cwb(3), op=ALU.mult),
             reads=[bxpi, bSP], writes=[baccP])
        yield
        P.op("dve", lambda e: e.tensor_tensor(out=ctmp, in0=xpi[:, :, 1:129], in1=cwb(1), op=ALU.mult),
             reads=[bxpi, bSP], writes=[bctmp])
        P.op("pool", lambda e: e.tensor_tensor(out=ctmpP, in0=xpi[:, :, 4:132], in1=cwb(4), op=ALU.mult),
             reads=[bxpi, bSP], writes=[bctmpP])
        yield
        P.op("dve", lambda e: e.tensor_tensor(out=acc, in0=acc, in1=ctmp, op=ALU.add), reads=[bacc, bctmp], writes=[bacc])
        P.op("pool", lambda e: e.tensor_tensor(out=accP, in0=accP, in1=ctmpP, op=ALU.add),
             reads=[baccP, bctmpP], writes=[baccP])
        yield
        P.op("dve", lambda e: e.tensor_tensor(out=ctmp, in0=xpi[:, :, 2:130], in1=cwb(2), op=ALU.mult),
             reads=[bxpi, bSP], writes=[bctmp])
        yield
        P.op("dve", lambda e: e.tensor_tensor(out=acc, in0=acc, in1=ctmp, op=ALU.add), reads=[bacc, bctmp], writes=[bacc])
        yield
        P.op("dve", lambda e: e.tensor_tensor(out=acc, in0=acc, in1=accP, op=ALU.add), reads=[bacc, baccP], writes=[bacc])
        for cc in range(12):
            P.op("act", lambda e, cc=cc: e.activation(out=xbc[s][:, cc, :], in_=acc[:, cc, :], func=AF.Silu,
                                                      bias=cb[:, cc:cc + 1]), reads=[bacc, bSP], writes=[bxbc[s]])
        yield

    deferred = []

    def flush():
        while deferred:
            deferred.pop(0)()

    def p1B(s, slot, fslot, bslot, store_dst):
        flush()
        cum_and_xs(s, slot, xbc[s], bxbc[s])
        yield
        P.op("dve", lambda e: e.tensor_tensor(out=te, in0=acum[s][:, 32:64], in1=acum[s][:, 0:32], op=ALU.subtract),
             reads=[bacum[s]], writes=[bte])
        P.op("act", lambda e: e.activation(out=te, in_=te, func=AF.Exp), reads=[bte], writes=[bte])
        P.op("act", lambda e: e.activation(out=dcy, in_=acum[s][:, 32:64], func=AF.Exp), reads=[bacum[s]], writes=[bdcy])
        P.op("dve", lambda e: e.tensor_tensor(out=wte, in0=dtall[:, slot, :], in1=te, op=ALU.mult),
             reads=[bdtall, bte], writes=[bwte])
        Bps = bank_bf(2)
        for g in range(2):
            P.op("pe", lambda e, g=g: e.transpose(Bps[:, g * 128:(g + 1) * 128], xbc[s][:, 8 + g, :], ident_bf),
                 reads=[bxbc[s], bConst], writes=[bk[2]])
        P.op("dve", lambda e: e.tensor_copy(out=Btok.rearrange("p g n -> p (g n)"), in_=Bps[:, 0:256]),
             reads=[bk[2]], writes=[bBtok])
        yield
        P.op("dve", lambda e: e.tensor_tensor(out=v3(xdtf[s]), in0=v3(xstok[s]), in1=bc16(wte[:, 0:16]), op=ALU.mult),
             reads=[bxs[s], bwte], writes=[bxdf[s]])
        P.op("pool", lambda e: e.tensor_tensor(out=v3(xdtb[s]), in0=v3(xstok[s]), in1=bc16(wte[:, 16:32]), op=ALU.mult),
             reads=[bxs[s], bwte], writes=[bxdb[s]])
        yield
        def smm(xd, bxd):
            for g in range(2):
                P.op("pe", lambda e, g=g, xd=xd: e.matmul(banks[3 + g][:, :], lhsT=Btok[:, g, :],
                                                         rhs=xd[:, g * 512:(g + 1) * 512], start=True, stop=True),
                     reads=[bBtok, bxd], writes=[bk[3 + g]])
        smm(xdtf[s], bxdf[s])
        yield
        if fslot is not None:
            P.op("act", lambda e: e.activation(out=prevF[:, fslot, :], in_=H[0], func=AF.Copy),
                 reads=[bH[0]], writes=[bpF[fslot]])
        P.op("pool", lambda e: e.tensor_tensor(out=v3(htmp), in0=v3(H[0]), in1=bc16(dcy[:, 0:16]), op=ALU.mult),
             reads=[bH[0], bdcy], writes=[bht])
        yield
        for g in range(2):
            P.op("dve", lambda e, g=g: e.tensor_tensor(out=H[0][:, g * 512:(g + 1) * 512],
                                                       in0=htmp[:, g * 512:(g + 1) * 512], in1=banks[3 + g][:, :],
                                                       op=ALU.add), reads=[bht, bk[3 + g]], writes=[bH[0]])
        smm(xdtb[s], bxdb[s])
        yield
        for g in range(2):
            P.op("act", lambda e, g=g: e.activation(out=prevB[:, bslot, g * 512:(g + 1) * 512], in_=banks[3 + g][:, :],
                                                    func=AF.Copy), reads=[bk[3 + g]], writes=[bpB[bslot]])
        P.op("dve", lambda e: e.tensor_copy(out=dcyB[:, bslot, :], in_=dcy[:, 16:32]), reads=[bdcy], writes=[bdB[bslot]])
        if store_dst is not None:
            deferred.append(lambda: P.op("sp", lambda e: e.dma_start(out=store_dst, in_=xbc[s]), reads=[bxbc[s]],
                                         dsem=("mxs", s)))
        yield

    def bwd_step(cur, slot, keep_prev):
        nxt = 3 - cur
        P.op("pool", lambda e: e.tensor_tensor(out=v3(htmp), in0=v3(H[cur]), in1=bc16(dcyB[:, slot, :]), op=ALU.mult),
             reads=[bH[cur], bdB[slot]], writes=[bht])
        P.op("dve", lambda e: e.tensor_tensor(out=H[nxt], in0=htmp, in1=prevB[:, slot, :], op=ALU.add),
             reads=[bht, bpB[slot]], writes=[bH[nxt]])
        if keep_prev:
            P.op("act", lambda e: e.activation(out=prevB[:, slot, :], in_=H[cur], func=AF.Copy),
                 reads=[bH[cur], bH[nxt]], writes=[bpB[slot]])
        return nxt

    def p2load(bi, ci):
        t0 = ci * 128
        l = ci % 3
        P.op("sp", lambda e: e.dma_start(out=xbc3[l], in_=c["xpost"][bi][:, :, t0:t0 + 128].rearrange("c p t -> p c t")),
             writes=[bxbc3[l]], dsem=("mx1", l))
        P.op("sp", lambda e: e.dma_start(out=ut[l], in_=c["uTs"][bi][:, :, t0:t0 + 128].rearrange("c p t -> p c t")),
             writes=[but[l]], dsem=("mx3", l))
        P.op("sp", lambda e: e.dma_start(out=vt[l], in_=c["vs"][bi][t0:t0 + 128, :]), writes=[bvt[l]], dsem=("mx4", l))
        P.op("sp", lambda e: e.dma_start(out=zt[l], in_=c["zs"][bi][t0:t0 + 128, :]), writes=[bzt[l]], dsem=("mx5", l))

    def p2A(bi, ci, s):
        l = ci % 3
        xb, bxb = xbc3[l], bxbc3[l]
        for d in range(2):
            tri = triL if d == 0 else triU
            for half in range(2):
                R, bR = Rb[d][half], bRb[d][half]
                P.op("dve" if d == 0 else "pool", lambda e, d=d, half=half, tri=tri, R=R: e.tensor_tensor(
                    out=R, in0=tri.unsqueeze(1).to_broadcast([128, 8, 128]),
                    in1=aall[:, ci, d * 16 + half * 8:d * 16 + half * 8 + 8].unsqueeze(2).to_broadcast([128, 8, 128]),
                    op=ALU.mult), reads=[baall, bSP], writes=[bR])
        cum_and_xs(s, ci, xb, bxb)
        P.op("act", lambda e: e.activation(out=E32[s], in_=acum[s][:, 0:32], func=AF.Exp), reads=[bacum[s]], writes=[bE32[s]])
        yield
        P.op("dve", lambda e: e.tensor_tensor(out=v3(xdtf[s]), in0=v3(xstok[s]), in1=bc16(dtall[:, ci, 0:16]), op=ALU.mult),
             reads=[bxs[s], bdtall], writes=[bxdf[s]])
        P.op("pool", lambda e: e.tensor_tensor(out=v3(xdtb[s]), in0=v3(xstok[s]), in1=bc16(dtall[:, ci, 16:32]), op=ALU.mult),
             reads=[bxs[s], bdtall], writes=[bxdb[s]])
        P.op("pool", lambda e: e.tensor_tensor(out=v3(xsd[s]), in0=v3(xstok[s]), in1=bc16(dsk), op=ALU.mult),
             reads=[bxs[s], bSP], writes=[bxsd[s]])
        for g in range(2):
            P.op("pe", lambda e, g=g: e.matmul(banks[0][:, 128 + g * 128:256 + g * 128], lhsT=xb[:, 8 + g, :],
                                               rhs=xb[:, 10 + g, :], start=True, stop=True),
                 reads=[bxb], writes=[bk[0]])
        for d, msk in enumerate((triL, triU)):
            P.op("dve", lambda e, d=d, msk=msk: e.tensor_tensor(
                out=cbTm[s][:, d], in0=banks[0][:, 128:384].rearrange("p (g q) -> p g q", q=128),
                in1=msk.unsqueeze(1).to_broadcast([128, 2, 128]), op=ALU.mult),
                reads=[bk[0], bSP], writes=[bcbT[s]])
        yield
        it = 0
        for d in range(2):
            msk = mgt if d == 0 else mlt
            eng = "dve" if d == 0 else "pool"
            for half in range(2):
                R, bR = Rb[d][half], bRb[d][half]
                for q4 in range(2):
                    bn = 2 + it % 2
                    P.op("pe", lambda e, bn=bn, q4=q4, msk=msk, R=R: e.matmul(
                        banks[bn][:, :], lhsT=msk, rhs=R[:, q4 * 4:(q4 + 1) * 4, :].rearrange("p h q -> p (h q)"),
                        start=True, stop=True), reads=[bR, bSP], writes=[bk[bn]])
                    eb = it % 2
                    P.op("act", lambda e, eb=eb, bn=bn: e.activation(out=Eb[:, eb, :], in_=banks[bn][:, :], func=AF.Exp),
                         reads=[bk[bn]], writes=[bEb[eb]])
                    mi = d * 4 + half * 2 + q4
                    P.op(eng, lambda e, eb=eb, mi=mi, d=d, half=half: e.tensor_tensor(
                        out=MT[s][:, mi * 4:(mi + 1) * 4, :], in0=Eb[:, eb, :].rearrange("p (h q) -> p h q", q=128),
                        in1=cbTm[s][:, d, half, :].unsqueeze(1).to_broadcast([128, 4, 128]), op=ALU.mult),
                        reads=[bEb[eb], bcbT[s]], writes=[bMT[s][mi]])
                    it += 1
                yield

    def p2B(bi, ci, s):
        t0 = ci * 128
        l = ci % 3
        flush()
        P.op("dve", lambda e: e.memset(ssq, 0.0), writes=[bssq])
        P.op("act", lambda e: e.activation(out=vhat, in_=vt[l], func=AF.Copy, accum_out=ssq[:, 2:3]),
             reads=[bvt[l]], writes=[bvh, bssq])
        for h in range(16):
            reg = banks[4 + h // 8][:, (h % 8) * 64:(h % 8 + 1) * 64]
            P.op("pe", lambda e, reg=reg, h=h: e.matmul(reg, lhsT=ident_bf, rhs=xsd[s][:, h * 64:(h + 1) * 64],
                                                        start=True, stop=False),
                 reads=[bConst, bxsd[s]], writes=[bk[4 + h // 8]])
            P.op("pe", lambda e, reg=reg, h=h: e.matmul(reg, lhsT=MT[s][:, h, :], rhs=xdtf[s][:, h * 64:(h + 1) * 64],
                                                        start=False, stop=False),
                 reads=[bMT[s][h // 4], bxdf[s]], writes=[bk[4 + h // 8]])
            P.op("pe", lambda e, reg=reg, h=h: e.matmul(reg, lhsT=MT[s][:, 16 + h, :], rhs=xdtb[s][:, h * 64:(h + 1) * 64],
                                                        start=False, stop=True),
                 reads=[bMT[s][4 + h // 4], bxdb[s]], writes=[bk[4 + h // 8]])
        yield
        n = 0
        for d in range(2):
            prev, bprev = (prevF, bpF) if d == 0 else (prevB, bpB)
            for g in range(2):
                bn = 6 + n % 2
                P.op("pe", lambda e, g=g, bn=bn, prev=prev: e.matmul(
                    banks[bn][:, :], lhsT=xbc3[l][:, 10 + g, :], rhs=prev[:, ci, g * 512:(g + 1) * 512],
                    start=True, stop=True), reads=[bxbc3[l], bprev[ci]], writes=[bk[bn]])
                dst = y1 if d == 0 else ytmp
                bdst = by1 if d == 0 else bytmp
                P.op("dve", lambda e, g=g, bn=bn, d=d, dst=dst: e.tensor_tensor(
                    out=dst[:, g * 512:(g + 1) * 512].rearrange("p (h d) -> p h d", d=64),
                    in0=banks[bn][:, :].rearrange("p (h d) -> p h d", d=64),
                    in1=E32[s][:, d * 16 + g * 8:d * 16 + g * 8 + 8].unsqueeze(2).to_broadcast([128, 8, 64]),
                    op=ALU.mult), reads=[bk[bn], bE32[s]], writes=[bdst])
                n += 1
            yield
        P.op("dve", lambda e: e.tensor_tensor(out=y1, in0=y1, in1=ytmp, op=ALU.add), reads=[by1, bytmp], writes=[by1])
        for g in range(2):
            P.op("dve", lambda e, g=g: e.tensor_tensor(out=y1[:, g * 512:(g + 1) * 512], in0=y1[:, g * 512:(g + 1) * 512],
                                                       in1=banks[4 + g][:, :], op=ALU.add),
                 reads=[by1, bk[4 + g]], writes=[by1])
        yield
        P.op("dve", lambda e: e.tensor_tensor(out=y1, in0=y1, in1=zt[l], op=ALU.mult), reads=[by1, bzt[l]], writes=[by1])
        for g in range(2):
            P.op("act", lambda e, g=g: e.activation(out=ytmp[:, g * 512:(g + 1) * 512], in_=y1[:, g * 512:(g + 1) * 512],
                                                    func=AF.Square, accum_out=ssq[:, g:g + 1]),
                 reads=[by1], writes=[bytmp, bssq])
        yield
        P.op("act", lambda e: e.activation(out=ytmp, in_=vt[l], func=AF.Square, accum_out=ssq[:, 3:4]),
             reads=[bvt[l]], writes=[bytmp, bssq])
        yield
        P.op("dve", lambda e: e.tensor_scalar(out=ssq[:, 4:5], in0=ssq[:, 2:3], scalar1=1.0 / 1024.0, scalar2=None,
                                              op0=ALU.mult), reads=[bssq], writes=[bssq])
        P.op("dve", lambda e: e.tensor_tensor(out=ssq[:, 5:6], in0=ssq[:, 4:5], in1=ssq[:, 4:5], op=ALU.mult),
             reads=[bssq], writes=[bssq])
        P.op("dve", lambda e: e.scalar_tensor_tensor(out=ssq[:, 2:3], in0=ssq[:, 3:4], scalar=1.0 / 1024.0,
                                                     in1=ssq[:, 5:6], op0=ALU.mult, op1=ALU.subtract),
             reads=[bssq], writes=[bssq])
        P.op("dve", lambda e: e.tensor_scalar(out=ssq[:, 0:2], in0=ssq[:, 0:2], scalar1=1.0 / 512.0, scalar2=None,
                                              op0=ALU.mult), reads=[bssq], writes=[bssq])
        P.op("act", lambda e: e.activation(out=ssq[:, 0:3], in_=ssq[:, 0:3], func=AF.Ln, bias=EPS, scale=1.0),
             reads=[bssq], writes=[bssq])
        P.op("act", lambda e: e.activation(out=ssq[:, 0:3], in_=ssq[:, 0:3], func=AF.Exp, scale=-0.5),
             reads=[bssq], writes=[bssq])
        yield
        for g in range(2):
            P.op("dve", lambda e, g=g: e.tensor_scalar(out=ssdo[:, g * 512:(g + 1) * 512], in0=y1[:, g * 512:(g + 1) * 512],
                                                       scalar1=ssq[:, g:g + 1], scalar2=None, op0=ALU.mult),
                 reads=[by1, bssq], writes=[bssdo])
        sT_bf = bank_bf(6)
        for j in range(8):
            P.op("pe", lambda e, j=j: e.transpose(sT_bf[:, j * 128:(j + 1) * 128], ssdo[:, j * 128:(j + 1) * 128], ident_bf),
                 reads=[bssdo, bConst], writes=[bk[6]])
        P.op("dve", lambda e: e.tensor_tensor(out=mixc[:, 8:16, :],
                                              in0=sT_bf[:, 0:1024].rearrange("p (j t) -> p j t", t=128),
                                              in1=ssdg.unsqueeze(2).to_broadcast([128, 8, 128]), op=ALU.mult),
             reads=[bk[6], bSP], writes=[bmixc])
        yield
        P.op("dve", lambda e: e.tensor_scalar(out=vhat, in0=vt[l], scalar1=ssq[:, 4:5], scalar2=ssq[:, 2:3],
                                              op0=ALU.subtract, op1=ALU.mult), reads=[bvt[l], bssq], writes=[bvh])
        for h in range(8):
            P.op("pe", lambda e, h=h: e.matmul(banks[4 + h // 4][:, (h % 4) * 128:(h % 4 + 1) * 128],
                                               lhsT=vhat[:, h * 128:(h + 1) * 128], rhs=wsT[:, h, :],
                                               start=True, stop=True), reads=[bvh, bws], writes=[bk[4 + h // 4]])
        yield
        for h in range(8):
            P.op("dve", lambda e, h=h: e.scalar_tensor_tensor(
                out=tt2[:, h, :], in0=banks[4 + h // 4][:, (h % 4) * 128:(h % 4 + 1) * 128], scalar=gng[:, h:h + 1],
                in1=Bc[:, h, :], op0=ALU.mult, op1=ALU.add), reads=[bk[4 + h // 4], bBc, bSP], writes=[bytmp])
        P.op("dve", lambda e: e.tensor_tensor(out=mixc[:, 0:8, :], in0=ut[l], in1=tt2, op=ALU.mult),
             reads=[but[l], bytmp], writes=[bmixc])
        deferred.append(lambda: P.op("sp", lambda e: e.dma_start(
            out=c["mixs"][bi][:, :, t0:t0 + 128].rearrange("c p t -> p c t"), in_=mixc), reads=[bmixc], dsem="mxo"))
        yield

    for bi in range(2):
        dtprep(bi)
        P.op("dve", lambda e: e.memset(H[0], 0.0), writes=[bH[0]])
        P.op("dve", lambda e: e.memset(H[1], 0.0), writes=[bH[1]])
        ch = []
        for ci in range(2):
            t0 = ci * 128
            ch.append((c["xprec"][bi][:, :, t0:t0 + 132].rearrange("c p t -> p c t"), 16 + ci, None, ci, None))
        for ci in range(NCH):
            t0 = ci * 128
            ch.append((c["xpre"][bi][:, :, t0:t0 + 132].rearrange("c p t -> p c t"), ci, ci, ci,
                       c["xpost"][bi][:, :, t0:t0 + 128].rearrange("c p t -> p c t")))
        cur = 1
        p1load(0, ch[0][0])
        for i in range(len(ch) + 1):
            if i + 1 < len(ch):
                p1load(i + 1, ch[i + 1][0])
            gA = p1A(i % 2, xp[i % 2], bxp[i % 2]) if i < len(ch) else None
            gB = None
            if i >= 1:
                src, slot, fslot, bslot, dst = ch[i - 1]
                gB = p1B((i - 1) % 2, slot, fslot, bslot, dst)
            _interleave(gA, gB)
            if i == 2:
                for cj in (1, 0):
                    cur = bwd_step(cur, cj, False)
        flush()
        for ci in range(NCH - 1, -1, -1):
            cur = bwd_step(cur, ci, True)
        P.barrier()
        p2load(bi, 0)
        for i in range(NCH + 1):
            if i + 1 < NCH:
                p2load(bi, i + 1)
            gA = p2A(bi, i, i % 2) if i < NCH else None
            gB = p2B(bi, i - 1, (i - 1) % 2) if i >= 1 else None
            _interleave(gA, gB)
        flush()
        P.barrier()


def _tile_cols(w, nch):
    return np.ascontiguousarray(w.reshape(16, 128, nch, 128).transpose(2, 1, 0, 3))


def prep_shared(inp):
    f = np.float32
    sh = {}
    sh["wmod"] = _tile_cols(np.asarray(inp["w_mod"], f)[0], 144)
    for i, k in ((1, "ffn1"), (2, "ffn2")):
        sh["wg%d" % i] = _tile_cols(np.asarray(inp[k + "_w_gate"], f)[0], NFF)
        sh["wu%d" % i] = _tile_cols(np.asarray(inp[k + "_w_up"], f)[0], NFF)
        sh["wd%d" % i] = np.ascontiguousarray(np.asarray(inp[k + "_w_down"], f)[0].reshape(NFF, 128, D))
    win = np.asarray(inp["w_in"], f)[0]
    fm = np.concatenate([win[:, 0:1024], win[:, 3072:4608]], axis=1)
    tm = win[:, 1024:3072]
    sh["winfm"] = _tile_cols(fm, 20)
    sh["wintm"] = _tile_cols(tm, 16)
    sh["windt"] = np.ascontiguousarray(win[:, 4608:4640].reshape(16, 128, 32).transpose(1, 0, 2))
    sh["wout"] = np.ascontiguousarray(np.asarray(inp["w_out"], f)[0].reshape(16, 128, D))
    mx = np.zeros((128, 2048), f)
    mx[:, 0:1024] = np.broadcast_to(np.asarray(inp["gmlp_b_s"], f)[0].reshape(1, 1024), (128, 1024))
    mx[:, 1024:2048] = np.asarray(inp["gmlp_w_s"], f)[0].transpose(2, 0, 1).reshape(128, 1024)
    sh["mx"] = mx
    return sh


def _fm(vec, n):
    return np.asarray(vec, np.float32).reshape(n, 128).T


def prep_core(inp, core):
    f = np.float32
    b0 = 2 * core
    x = np.asarray(inp["x"], f)
    ctx = np.asarray(inp["ctx"], f)
    d = {}
    d["xT"] = np.ascontiguousarray(x[b0:b0 + 2].transpose(0, 2, 1).reshape(2, FC, 128, SEQ))
    d["ctxT"] = np.ascontiguousarray(ctx[b0:b0 + 2].transpose(2, 0, 1).reshape(FC, 128, 2 * CTX))
    sp = np.zeros((128, SPW), f)
    for i, k in enumerate(("norm_ffn1", "norm_mix", "norm_ffn2")):
        sp[:, O_NG + 16 * i:O_NG + 16 * i + 16] = _fm(np.asarray(inp[k])[0], 16)
    sp[:, O_NG + 48:O_NG + 64] = _fm(np.asarray(inp["norm_final"]), 16)
    sp[:, O_BMOD:O_BMOD + 144] = _fm(np.asarray(inp["b_mod"])[0], 144)
    cwt = np.asarray(inp["conv_w"], f)[0]
    sp[:, O_CW:O_CW + 60] = cwt.T.reshape(12, 128, 5).transpose(1, 0, 2).reshape(128, 60)
    sp[:, O_CB:O_CB + 12] = _fm(np.asarray(inp["conv_b"])[0], 12)
    sp[:, O_DTB:O_DTB + 32] = np.asarray(inp["dt_bias"], f)[0].reshape(1, 32)
    sp[:, O_ALOG:O_ALOG + 32] = np.asarray(inp["a_log"], f)[0].reshape(1, 32)
    sp[:, O_DSK:O_DSK + 16] = np.asarray(inp["d_skip"], f)[0].reshape(1, 16)
    sp[:, O_SSDG:O_SSDG + 8] = _fm(np.asarray(inp["ssd_norm_g"])[0], 8)
    sp[:, O_GNG:O_GNG + 8] = _fm(np.asarray(inp["gmlp_norm_g"])[0], 8)
    sp[:, O_GNB:O_GNB + 8] = _fm(np.asarray(inp["gmlp_norm_b"])[0], 8)
    c = np.asarray(inp["c"], f)
    rows = np.stack([c[b0], c[b0 + 1], np.asarray(inp["c_ctx"], f)], axis=1)
    sp[:, O_CT:O_CT + 48] = rows.reshape(16, 128, 3).transpose(1, 0, 2).reshape(128, 48)
    ar = np.arange(128)
    ident = np.eye(128, dtype=f)
    triL = (ar[:, None] <= ar[None, :]).astype(f)
    triU = (ar[:, None] >= ar[None, :]).astype(f)
    mgt = (ar[:, None] > ar[None, :]).astype(f)
    mlt = (ar[:, None] < ar[None, :]).astype(f)
    sp[:, O_C128:O_C128 + 640] = np.stack([ident, triL, triU, mgt, mlt], axis=1).reshape(128, 640)
    d["sp"] = sp
    return d


_NC_CACHE = {}


def kernel(**inputs):
    n = 8
    if "nc" not in _NC_CACHE:
        _NC_CACHE["nc"] = build_program()
    nc = _NC_CACHE["nc"]
    sh = prep_shared(inputs)
    in_maps = []
    for core in range(n):
        m = dict(sh)
        m.update(prep_core(inputs, core))
        in_maps.append(m)
    res = run_bass_kernel_spmd(nc, in_maps, core_ids=list(range(n)))
    out = np.empty((16, SEQ, D), np.float32)
    for core in range(n):
        o = np.asarray(res.results[core]["outT"]).reshape(2, D, SEQ)
        out[2 * core:2 * core + 2] = o.transpose(0, 2, 1)
    return out
```
